# Optimizing a Trainium2 kernel written in Bass

```python
import jax, jax.numpy as jnp
from jax import lax
import numpy as np

D_MODEL = 1024
BATCH = 8
SEQ = 2048
DEPTH = 2

CTX_LEN = 256
GRID_W = 64
NORM_EPS = 1e-6

N_HEADS = 8
N_KV_HEADS = 2
HEAD_DIM = 64
Q_GROUP = N_HEADS // N_KV_HEADS
Q_WIDTH = N_HEADS * HEAD_DIM
KV_WIDTH = N_KV_HEADS * HEAD_DIM
ROPE_THETA = 10000.0
AXIS_PAIRS = HEAD_DIM // 4
Q_BLOCK = 128

SC_WIDTH = 256
FOURIER_GROUPS = 4
FOURIER_WIDTH = 256
FOURIER_GROUP = FOURIER_WIDTH // FOURIER_GROUPS
POOL_WINDOWS = (2, 4, 8, 16)
POOL_WIDTH = 256
POOL_GROUP = POOL_WIDTH // len(POOL_WINDOWS)

N_BRANCHES = 4
D_FF = 2816

OFF_Q = 0
OFF_K = OFF_Q + Q_WIDTH
OFF_V = OFF_K + KV_WIDTH
OFF_SC = OFF_V + KV_WIDTH
OFF_F = OFF_SC + 3 * SC_WIDTH
OFF_P = OFF_F + FOURIER_WIDTH
OFF_G = OFF_P + POOL_WIDTH
IN_WIDTH = OFF_G + N_BRANCHES * D_MODEL

kernel_name = "hybrid_parallel_gated_diffusion_block"


def rmsnorm(x, g):
    xf = x.astype(jnp.float32)
    y = xf * lax.rsqrt(jnp.mean(xf * xf, axis=-1, keepdims=True) + NORM_EPS)
    return (y * g.astype(jnp.float32)).astype(x.dtype)


def adaln(cvec, w_mod, b_mod):
    m = jax.nn.silu(cvec) @ w_mod + b_mod
    return m.reshape(cvec.shape[0], 6, D_MODEL)


def modulate(h, shift, scale):
    return h * (1.0 + scale[:, None, :]) + shift[:, None, :]


def dwconv3(x, w):
    xp = jnp.pad(x, ((0, 0), (1, 1), (0, 0)))
    return xp[:, :-2] * w[0] + xp[:, 1:-1] * w[1] + xp[:, 2:] * w[2]


def axial_rope(n):
    rows = n // GRID_W
    row = jnp.repeat(jnp.arange(rows), GRID_W).astype(jnp.float32)
    col = jnp.tile(jnp.arange(GRID_W), rows).astype(jnp.float32)
    inv = ROPE_THETA ** (-jnp.arange(AXIS_PAIRS, dtype=jnp.float32) / AXIS_PAIRS)
    ang = jnp.stack([row[:, None] * inv, col[:, None] * inv], axis=1)
    ang = ang[None, :, None, :, None, :]
    return jnp.cos(ang), jnp.sin(ang)


def apply_rope(x, cos, sin):
    b, n, h, _ = x.shape
    xr = x.astype(jnp.float32).reshape(b, n, h, 2, 2, AXIS_PAIRS)
    rot = jnp.concatenate([-xr[..., 1:2, :], xr[..., 0:1, :]], axis=-2)
    return (xr * cos + rot * sin).reshape(b, n, h, HEAD_DIM).astype(x.dtype)


def kv_heads(z_kv, k_gain):
    b, n, _ = z_kv.shape
    k = rmsnorm(z_kv[..., :KV_WIDTH].reshape(b, n, N_KV_HEADS, HEAD_DIM), k_gain)
    v = z_kv[..., KV_WIDTH:].reshape(b, n, N_KV_HEADS, HEAD_DIM)
    return k, v


def attend_blocked(q, k, v):
    b, n, _, _ = q.shape
    nb = n // Q_BLOCK
    qb = q.reshape(b, nb, Q_BLOCK, N_KV_HEADS, Q_GROUP, HEAD_DIM).transpose(1, 0, 2, 3, 4, 5)
    scale = HEAD_DIM ** -0.5

    def one_block(qq):
        s = jnp.einsum("bqkgd,bskd->bkgqs", qq, k).astype(jnp.float32) * scale
        p = jax.nn.softmax(s, axis=-1).astype(v.dtype)
        return jnp.einsum("bkgqs,bskd->bqkgd", p, v)

    o = lax.map(one_block, qb)
    return o.transpose(1, 0, 2, 3, 4, 5).reshape(b, n, Q_WIDTH)


def pool_minus_identity(x):
    _, n, _ = x.shape
    xf = x.astype(jnp.float32)
    cs = jnp.pad(jnp.cumsum(xf, axis=1), ((0, 0), (1, 0), (0, 0)))
    t = jnp.arange(n)
    outs = []
    for gi, w in enumerate(POOL_WINDOWS):
        left, right = (w - 1) // 2, w // 2
        lo = jnp.maximum(t - left, 0)
        hi = jnp.minimum(t + right + 1, n)
        sl = slice(gi * POOL_GROUP, (gi + 1) * POOL_GROUP)
        csg = cs[..., sl]
        cnt = (hi - lo).astype(jnp.float32)[None, :, None]
        mean = (jnp.take(csg, hi, axis=1) - jnp.take(csg, lo, axis=1)) / cnt
        outs.append(mean - xf[..., sl])
    return jnp.concatenate(outs, axis=-1).astype(x.dtype)


def token_mixers(z, k, v, q_gain, conv_sc, pool_mat, pool_scale,
                 w_br_attn, w_br_sc, w_br_f, w_br_p, w_out, rope, ctx_kv):
    b, n, _ = z.shape
    q = rmsnorm(z[..., OFF_Q:OFF_K].reshape(b, n, N_HEADS, HEAD_DIM), q_gain)
    if rope is not None:
        cos, sin = rope
        q = apply_rope(q, cos, sin)
        k = apply_rope(k, cos, sin)
    if ctx_kv is not None:
        k = jnp.concatenate([k, ctx_kv[0]], axis=1)
        v = jnp.concatenate([v, ctx_kv[1]], axis=1)
    o_attn = attend_blocked(q, k, v) @ w_br_attn
    zs = z[..., OFF_SC:OFF_F]
    gb, gc, xs = zs[..., :SC_WIDTH], zs[..., SC_WIDTH:2 * SC_WIDTH], zs[..., 2 * SC_WIDTH:]
    o_sc = (gb * dwconv3(gc * xs, conv_sc)) @ w_br_sc
    xg = z[..., OFF_F:OFF_P].astype(jnp.float32).reshape(b, n, FOURIER_GROUPS, FOURIER_GROUP)
    xfour = jnp.fft.fft2(xg, axes=(1, 3), norm="ortho").real.astype(z.dtype).reshape(b, n, FOURIER_WIDTH)
    o_f = xfour @ w_br_f
    pooled = pool_minus_identity(z[..., OFF_P:OFF_G]).reshape(b, n, len(POOL_WINDOWS), POOL_GROUP)
    yp = jnp.einsum("bngc,gcd->bngd", pooled, pool_mat).reshape(b, n, POOL_WIDTH) * pool_scale
    o_p = yp @ w_br_p
    g = jax.nn.sigmoid(z[..., OFF_G:].reshape(b, n, N_BRANCHES, D_MODEL))
    y = g[:, :, 0] * o_attn + g[:, :, 1] * o_sc + g[:, :, 2] * o_f + g[:, :, 3] * o_p
    return y @ w_out


def conv_ffn(h, w_up, conv_w, w_down):
    u = dwconv3(h @ w_up, conv_w)
    a, bval = u[..., :D_FF], u[..., D_FF:]
    return (jax.nn.silu(a) * bval) @ w_down


def setup_inputs(seed: int = 0) -> dict:
    key = jax.random.key(seed)
    ks = jax.random.split(key, 24)
    f32 = jnp.float32

    def nrm(k, shape, scale):
        return jax.random.normal(k, shape, f32) * scale

    L, D = DEPTH, D_MODEL
    return {
        "x": nrm(ks[0], (BATCH, SEQ, D), 1.0),
        "c": nrm(ks[1], (BATCH, D), 1.0),
        "ctx": nrm(ks[2], (BATCH, CTX_LEN, D), 1.0),
        "c_ctx": nrm(ks[3], (D,), 1.0),
        "w_mod": nrm(ks[4], (L, D, 6 * D), 0.5 * D ** -0.5),
        "b_mod": nrm(ks[5], (L, 6 * D), 0.02),
        "norm1": 1.0 + nrm(ks[6], (L, D), 0.02),
        "norm2": 1.0 + nrm(ks[7], (L, D), 0.02),
        "w_in": nrm(ks[8], (L, D, IN_WIDTH), D ** -0.5),
        "conv_sc": nrm(ks[9], (L, 3, SC_WIDTH), 3 ** -0.5),
        "qk_gain": 1.0 + nrm(ks[10], (L, 2, HEAD_DIM), 0.02),
        "pool_mat": nrm(ks[11], (L, len(POOL_WINDOWS), POOL_GROUP, POOL_GROUP), POOL_GROUP ** -0.5),
        "pool_scale": 1.0 + nrm(ks[12], (L, POOL_WIDTH), 0.02),
        "w_br_attn": nrm(ks[13], (L, Q_WIDTH, D), Q_WIDTH ** -0.5),
        "w_br_sc": nrm(ks[14], (L, SC_WIDTH, D), SC_WIDTH ** -0.5),
        "w_br_f": nrm(ks[15], (L, FOURIER_WIDTH, D), FOURIER_WIDTH ** -0.5),
        "w_br_p": nrm(ks[16], (L, POOL_WIDTH, D), POOL_WIDTH ** -0.5),
        "w_out": nrm(ks[17], (L, D, D), D ** -0.5),
        "w_up": nrm(ks[18], (L, D, 2 * D_FF), D ** -0.5),
        "w_conv_ffn": nrm(ks[19], (L, 3, 2 * D_FF), 3 ** -0.5),
        "w_down": nrm(ks[20], (L, D_FF, D), D_FF ** -0.5),
        "final_norm": 1.0 + nrm(ks[21], (D,), 0.02),
    }


def reference(x, c, ctx, c_ctx, w_mod, b_mod, norm1, norm2, w_in, conv_sc, qk_gain, pool_mat,
              pool_scale, w_br_attn, w_br_sc, w_br_f, w_br_p, w_out, w_up, w_conv_ffn, w_down, final_norm):
    rope = axial_rope(x.shape[1])
    lat, cx = x, ctx
    for l in range(DEPTH):
        m_lat = adaln(c, w_mod[l], b_mod[l])
        m_ctx = adaln(c_ctx[None], w_mod[l], b_mod[l])
        q_gain, k_gain = qk_gain[l, 0], qk_gain[l, 1]
        mixer_params = (q_gain, conv_sc[l], pool_mat[l], pool_scale[l],
                        w_br_attn[l], w_br_sc[l], w_br_f[l], w_br_p[l], w_out[l])
        last = l == DEPTH - 1

        hc = modulate(rmsnorm(cx, norm1[l]), m_ctx[:, 0], m_ctx[:, 1])
        if last:
            kc, vc = kv_heads(hc @ w_in[l][:, OFF_K:OFF_SC], k_gain)
        else:
            zc = hc @ w_in[l]
            kc, vc = kv_heads(zc[..., OFF_K:OFF_SC], k_gain)

        hl = modulate(rmsnorm(lat, norm1[l]), m_lat[:, 0], m_lat[:, 1])
        zl = hl @ w_in[l]
        kl, vl = kv_heads(zl[..., OFF_K:OFF_SC], k_gain)
        lat = lat + m_lat[:, 2][:, None, :] * token_mixers(zl, kl, vl, *mixer_params, rope, (kc, vc))
        hf = modulate(rmsnorm(lat, norm2[l]), m_lat[:, 3], m_lat[:, 4])
        lat = lat + m_lat[:, 5][:, None, :] * conv_ffn(hf, w_up[l], w_conv_ffn[l], w_down[l])

        if not last:
            cx = cx + m_ctx[:, 2][:, None, :] * token_mixers(zc, kc, vc, *mixer_params, None, None)
            hfc = modulate(rmsnorm(cx, norm2[l]), m_ctx[:, 3], m_ctx[:, 4])
            cx = cx + m_ctx[:, 5][:, None, :] * conv_ffn(hfc, w_up[l], w_conv_ffn[l], w_down[l])
    return rmsnorm(lat, final_norm)
```

```python
import contextlib
import numpy as np
import ml_dtypes
import concourse.bass as bass
import concourse.mybir as mybir
from concourse.bass_utils import run_bass_kernel_spmd

F32 = mybir.dt.float32
BF16 = mybir.dt.bfloat16
AF = mybir.ActivationFunctionType
ALU = mybir.AluOpType

D = 1024
SEQ = 2048
CTX = 256
NT = SEQ + CTX
DEPTH = 2
DFF = 2816
NFF = 22
EPS = 1e-6
GSZ = 4
NGRP = 6

_off = {}
_ns = 0


def _alloc(name, n):
    global _ns
    _off[name] = (_ns, n)
    _ns += n


_alloc("cc", 16)
_alloc("bmod", 2 * 48)
_alloc("norm1", 16)
_alloc("norm2", 16)
_alloc("fnorm", 8)
_alloc("convsc", 2 * 2 * 3)
_alloc("qkg", 2 * 2)
_alloc("pscale", 2 * 2)
_alloc("wcf", 2 * 44 * 3)
NS = _ns

CB_ID = 0
CB_ONES = 128
CB_BONES = 256
CB_ROT = 384
CB_BDCS = 512
CB_BAND = 768
CB_ONE = CB_BAND + 20 * 128
NCB = CB_ONE + 128

OFF_Q, OFF_K, OFF_V, OFF_SC, OFF_F, OFF_P, OFF_G = 0, 512, 640, 768, 1536, 1792, 2048


def _win_cols():
    cols = []
    cols += list(range(OFF_K, OFF_K + 128))
    cols += list(range(OFF_V, OFF_V + 128))
    cols += list(range(OFF_F, OFF_F + 256))
    for j in range(2):
        for part in range(3):
            s = OFF_SC + part * 256 + j * 128
            cols += list(range(s, s + 128))
    cols += list(range(OFF_P, OFF_P + 256))
    for j in range(4):
        cols += list(range(OFF_Q + j * 64, OFF_Q + j * 64 + 64))
        cols += list(range(OFF_Q + (4 + j) * 64, OFF_Q + (4 + j) * 64 + 64))
    for c in range(8):
        for i in range(4):
            s = OFF_G + i * 1024 + c * 128
            cols += list(range(s, s + 128))
    return np.array(cols)


def _blk(w, cb):
    K, C = w.shape
    return np.ascontiguousarray(w.reshape(K // 128, 128, C // cb, cb).transpose(2, 1, 0, 3))


def _bf(a):
    return np.ascontiguousarray(a.astype(ml_dtypes.bfloat16))


def _consts():
    cb = np.zeros((128, NCB), np.float32)
    cb[:, CB_ID:CB_ID + 128] = np.eye(128)
    cb[:, CB_ONES:CB_ONES + 128] = 1.0 / 1024
    for h in range(2):
        cb[h * 64:(h + 1) * 64, CB_BONES + h * 64:CB_BONES + (h + 1) * 64] = 1.0 / 64
    cb[:, CB_ONE:CB_ONE + 128] = 1.0
    for p in range(128):
        half = (p // 16) % 2
        if half == 0:
            cb[p + 16, CB_ROT + p] = -1.0
        else:
            cb[p - 16, CB_ROT + p] = 1.0
    cidx = np.arange(64)
    ph = 2 * np.pi * np.outer(cidx, cidx) / 64.0
    for g in range(2):
        cb[g * 64:(g + 1) * 64, CB_BDCS + g * 64:CB_BDCS + (g + 1) * 64] = np.cos(ph)
        cb[g * 64:(g + 1) * 64, CB_BDCS + 128 + g * 64:CB_BDCS + 128 + (g + 1) * 64] = np.sin(ph)
    n = 384
    t = np.arange(n)
    for wi, w in enumerate((2, 4, 8, 16)):
        left, right = (w - 1) // 2, w // 2
        lo = np.maximum(t - left, 0)
        hi = np.minimum(t + right + 1, n)
        P = np.zeros((n, n), np.float64)
        for q in range(n):
            P[lo[q]:hi[q], q] = 1.0 / (hi[q] - lo[q])
            P[q, q] -= 1.0
        mats = [P[0:128, 128:256], P[128:256, 128:256], P[256:384, 128:256], P[0:128, 0:128], P[256:384, 256:384]]
        for v, m in enumerate(mats):
            o = CB_BAND + (wi * 5 + v) * 128
            cb[:, o:o + 128] = m
    inv = 10000.0 ** (-np.arange(16, dtype=np.float32) / 16)
    pos = np.arange(SEQ)
    row = (pos // 64).astype(np.float32)
    col = (pos % 64).astype(np.float32)
    p = np.arange(128)
    d = p % 64
    axis = d // 32
    pair = d % 16
    posax = np.where(axis[:, None] == 0, row[None, :], col[None, :]).astype(np.float32)
    ang = (posax * inv[pair][:, None]).astype(np.float32)
    rope = np.stack([np.cos(ang), np.sin(ang)], axis=1)
    rope = rope.reshape(128, 2, 4, 512).transpose(2, 0, 1, 3)
    def dft(N, wj):
        nn = np.arange(N, dtype=np.int64)
        prod = (np.outer(nn, nn) % N).astype(np.float64) * (2 * np.pi / N)
        sc = 1.0 / np.sqrt(64.0 * N)
        C = np.cos(prod) * sc
        S = -np.sin(prod) * sc
        T = np.stack([C, S], axis=1)
        T = T.reshape(N // 128, 128, 2, N // wj, wj).transpose(3, 0, 1, 2, 4)
        return T
    dftl = dft(SEQ, 512)
    dftc = dft(CTX, 256)[0]
    return _bf(cb), _bf(rope), _bf(dftl), _bf(dftc)


def _prep_shared(inp):
    f = lambda a: np.asarray(a, dtype=np.float32)
    cols = _win_cols()
    sh = {}
    sh["wmod"] = np.stack([_blk(f(inp["w_mod"][l]), 256) for l in range(DEPTH)])
    sh["win"] = np.stack([_blk(f(inp["w_in"][l])[:, cols], 256) for l in range(DEPTH)])
    hperm = []
    for j in range(4):
        hperm += list(range(j * 64, j * 64 + 64)) + list(range((4 + j) * 64, (4 + j) * 64 + 64))
    wbr = []
    for l in range(DEPTH):
        w = np.concatenate([f(inp["w_br_attn"][l])[hperm], f(inp["w_br_sc"][l]), f(inp["w_br_f"][l]),
                            f(inp["w_br_p"][l])], axis=0)
        wbr.append(_blk(w, 128))
    sh["wbr"] = np.stack(wbr)
    sh["wout"] = np.ascontiguousarray(f(inp["w_out"]).reshape(DEPTH, 8, 128, 1024))
    wup = []
    for l in range(DEPTH):
        w = f(inp["w_up"][l])
        a = w[:, :DFF].reshape(D, NFF, 128)
        b = w[:, DFF:].reshape(D, NFF, 128)
        ab = np.concatenate([a, b], axis=2).reshape(D, NFF * 256)
        wup.append(_blk(ab, 256))
    sh["wup"] = np.stack(wup)
    wdn = np.zeros((DEPTH, NGRP, 128, GSZ, 1024), np.float32)
    for l in range(DEPTH):
        w = f(inp["w_down"][l]).reshape(NFF, 128, 1024)
        for j in range(NFF):
            wdn[l, j // GSZ, :, j % GSZ, :] = w[j]
    sh["wdn"] = wdn
    cb, rope, dftl, dftc = _consts()
    sh["cb"] = cb
    sh["rope"] = rope
    sh["dftl"] = dftl
    sh["dftc"] = dftc
    sh["identf"] = np.eye(128, dtype=np.float32)
    return sh


def _smalls(inp, b):
    f = lambda a: np.asarray(a, dtype=np.float32)
    s = np.zeros((128, NS), np.float32)

    def put(name, arr):
        o, n = _off[name]
        s[:, o:o + n] = arr.reshape(128, n)

    cc = np.stack([f(inp["c"])[b].reshape(8, 128).T, f(inp["c_ctx"]).reshape(8, 128).T], axis=2)
    put("cc", cc)
    put("bmod", f(inp["b_mod"]).reshape(DEPTH, 48, 128).transpose(2, 0, 1))
    put("norm1", f(inp["norm1"]).reshape(DEPTH, 8, 128).transpose(2, 0, 1))
    put("norm2", f(inp["norm2"]).reshape(DEPTH, 8, 128).transpose(2, 0, 1))
    put("fnorm", f(inp["final_norm"]).reshape(8, 128).T)
    put("convsc", f(inp["conv_sc"]).reshape(DEPTH, 3, 2, 128).transpose(3, 0, 2, 1))
    qk = f(inp["qk_gain"])
    put("qkg", np.tile(qk.transpose(2, 0, 1), (2, 1, 1)))
    put("pscale", f(inp["pool_scale"]).reshape(DEPTH, 2, 128).transpose(2, 0, 1))
    put("wcf", f(inp["w_conv_ffn"]).reshape(DEPTH, 3, 44, 128).transpose(3, 0, 2, 1))
    pm = f(inp["pool_mat"])
    bd = np.zeros((128, DEPTH, 2, 128), np.float32)
    for l in range(DEPTH):
        for ch in range(2):
            for g in range(2):
                bd[g * 64:(g + 1) * 64, l, ch, g * 64:(g + 1) * 64] = pm[l, 2 * ch + g]
    return s, np.ascontiguousarray(bd.reshape(128, DEPTH * 256))


def _flat(x):
    out = []
    st = [x]
    while st:
        a = st.pop()
        if isinstance(a, (tuple, list, set)):
            st.extend(a)
        else:
            out.append(a)
    return out


class Buf:
    __slots__ = ("name", "w", "r", "excl")

    def __init__(self, name, excl=False):
        self.name = name
        self.w = None
        self.r = []
        self.excl = excl


class Op:
    __slots__ = ("eng", "fn", "deps", "is_dma", "sem", "val", "needed", "prev_val")


class Prog:
    ENGS = ("pe", "act", "dve", "pool", "sp")

    def __init__(self):
        self.ops = {e: [] for e in self.ENGS}

    def op(self, eng, fn, reads=(), writes=(), dma=False):
        o = Op()
        o.eng = eng
        o.fn = fn
        o.is_dma = dma
        o.needed = False
        o.sem = None
        o.val = 0
        o.prev_val = 0
        reads = _flat(reads)
        writes = _flat(writes)
        ex = [b for b in reads if b.excl]
        if ex:
            reads = [b for b in reads if not b.excl]
            writes = writes + ex
        deps = {}
        for b in reads:
            if b.w is not None:
                deps[id(b.w)] = b.w
        for b in writes:
            if b.w is not None:
                deps[id(b.w)] = b.w
            for r in b.r:
                deps[id(r)] = r
        dl = []
        for d in deps.values():
            if d is o:
                continue
            if d.eng == eng and not d.is_dma and eng == "pe":
                continue
            d.needed = True
            dl.append(d)
        o.deps = dl
        for b in reads:
            b.r.append(o)
        for b in writes:
            b.w = o
            b.r = []
        self.ops[eng].append(o)
        return o


FLAGS = set()


class _Stop(Exception):
    pass


def build_nc(dbg=None, limit=None):
    nc = bass.Bass("TRN2", target_bir_lowering=False)
    dram = {}

    def din(name, shape, dt=F32):
        dram[name] = nc.dram_tensor(name, list(shape), dt, kind="ExternalInput").ap()
        return dram[name]

    x_d = din("x", [SEQ, D])
    ctx_d = din("ctx", [CTX, D])
    smalls_d = din("smalls", [128, NS])
    bdpm_d = din("bdpm", [128, DEPTH * 256])
    cb_d = din("cb", [128, NCB], BF16)
    identf_d = din("identf", [128, 128])
    rope_d = din("rope", [4, 128, 2, 512], BF16)
    dftl_d = din("dftl", [4, 16, 128, 2, 512], BF16)
    dftc_d = din("dftc", [2, 128, 2, 256], BF16)
    wmod_d = din("wmod", [DEPTH, 24, 128, 8, 256])
    win_d = din("win", [DEPTH, 24, 128, 8, 256])
    wbr_d = din("wbr", [DEPTH, 8, 128, 10, 128])
    wout_d = din("wout", [DEPTH, 8, 128, 1024])
    wup_d = din("wup", [DEPTH, NFF, 128, 8, 256])
    wdn_d = din("wdn", [DEPTH, NGRP, 128, GSZ, 1024])
    out_d = nc.dram_tensor("out", [SEQ, D], F32, kind="ExternalOutput").ap()
    dbg_d = None
    if dbg is not None:
        dbg_d = nc.dram_tensor("dbg", [128, sum(n for _, n in dbg[0])], dbg[1], kind="ExternalOutput").ap()

    P = Prog()
    es = contextlib.ExitStack()
    with es:
        def sb(name, shape, dt):
            return es.enter_context(nc.sbuf_tensor(name, list(shape), dt))

        latT = sb("latT", [128, 8, SEQ], F32)
        cxT = sb("cxT", [128, 8, CTX], F32)
        hT = sb("hT", [128, 8, NT], BF16)
        big = sb("big", [128, 8, NT], BF16)
        pbuf = sb("pbuf", [128, 2, NT], BF16)
        KTt = sb("KTt", [128, NT + 2], BF16)
        Vt2 = sb("Vt2", [128, NT + 2], BF16)
        cbt = sb("cbt", [128, NCB], BF16)
        identf = sb("identf_sb", [128, 128], F32)
        smalls = sb("smalls_sb", [128, NS], F32)
        modT = sb("modT", [128, DEPTH, 48, 2], F32)
        vecs = sb("vecs", [128, 2, 6, 8], F32)
        bdpm = sb("bdpm_bf", [128, 2, 128], BF16)
        scb = sb("scb", [128, 8, 2], BF16)
        NRING = 4
        ring = sb("ring", [128, NRING, 2048], BF16)
        NTF = 2
        tf = sb("tf", [128, NTF, 512], F32)
        NRS = 2
        rs = sb("rs", [128, NRS, 512], F32)
        NTB = 2
        tb = sb("tb", [128, NTB, 512], BF16)
        NM = 3
        mreg = sb("mreg", [128, NM, 1024], BF16)
        small1 = sb("small1", [128, 8], F32)
        ytp = sb("ytp", [128, 2, 512], BF16)
        ytp_b = [Buf("ytp0"), Buf("ytp1")]
        epsc = sb("epsc", [128, 1], F32)
        banks = [es.enter_context(nc.psum_tensor(f"ps{i}", [128, 512], F32)) for i in range(8)]

        KT = KTt[:, 0:NT]
        Vt = Vt2[:, 0:NT].rearrange("p (t c) -> p t c", t=18)
        bigflat = big[:, :, :].rearrange("p a b -> p (a b)")

        B = {}

        def buf(name):
            if name not in B:
                B[name] = Buf(name)
            return B[name]

        bank_b = [Buf(f"bank{i}", True) for i in range(8)]
        ring_b = [Buf(f"ring{i}") for i in range(NRING)]
        tf_b = [Buf(f"tf{i}") for i in range(NTF)]
        rs_b = [Buf(f"rs{i}") for i in range(NRS)]
        tb_b = [Buf(f"tb{i}") for i in range(NTB)]
        mh_b = [Buf(f"mh{i}") for i in range(2 * NM)]
        m_b = [(mh_b[2 * i], mh_b[2 * i + 1]) for i in range(NM)]
        cnt = {"ring": 0, "tf": 0, "tb": 0, "m": 0, "bank": 0, "rs": 0, "pt": 0}

        def nxt(kind, n):
            i = cnt[kind] % n
            cnt[kind] += 1
            return i

        def tfa():
            i = nxt("tf", NTF)
            return tf[:, i, :], tf_b[i]

        def rsa():
            i = nxt("rs", NRS)
            return rs[:, i, :], rs_b[i]

        def tba():
            i = nxt("tb", NTB)
            return tb[:, i, :], tb_b[i]

        def ma():
            i = nxt("m", NM)
            return mreg[:, i, :], m_b[i]

        def pta():
            i = nxt("pt", 2 * NM)
            return mreg[:, i // 2, (i % 2) * 512:(i % 2) * 512 + 512], mh_b[i]

        def bka(pool=None):
            pool = pool if pool is not None else list(range(8))
            i = pool[cnt["bank"] % len(pool)]
            cnt["bank"] += 1
            return banks[i], bank_b[i]

        def ringa():
            i = nxt("ring", NRING)
            return ring[:, i, :], ring_b[i]

        SEGS = {"ctx": (0, CTX), "lat": (CTX, SEQ)}

        def tiles_of(seg):
            s0, n = SEGS[seg]
            if seg == "ctx":
                return [("ctx", 0, 256)]
            return [("lat", s0 + 512 * t, 512) for t in range(4)]

        def resid(seg, k, c0, W):
            if seg == "ctx":
                return cxT[:, k, c0:c0 + W]
            return latT[:, k, c0 - CTX:c0 - CTX + W]

        def rb(seg, k, c0):
            return buf(f"res_{seg}_{k}_{c0}")

        def hb(c0):
            return buf(f"hT_{c0}")

        def BG(a, b):
            return tuple(buf(f"bigf_{i}") for i in range(a // 256, (b - 1) // 256 + 1))

        def bigb(ch, c0):
            W_ = 256 if c0 == 0 else 512
            return BG(ch * NT + c0, ch * NT + c0 + W_)

        def KTB(c0):
            return buf(f"KT_{c0}")

        KT_ALL = tuple(KTB(c0) for c0 in (0, 256, 768, 1280, 1792))

        def pbb(ch, c0):
            return buf(f"pb_{ch}_{c0}")

        def sm(name, *idx):
            o, n = _off[name]
            return o

        def dma(eng, out, in_, reads=(), writes=()):
            h = {"sp": nc.sync, "pool": nc.gpsimd}[eng]
            return P.op(eng, lambda e, out=out, in_=in_: e.dma_start(out=out, in_=in_), reads, writes, dma=True)

        def mm(out, pairs, reads, writes, first=True, last=True):
            def fn(e, out=out, pairs=pairs, first=first, last=last):
                ins = None
                n = len(pairs)
                for i, (l, r) in enumerate(pairs):
                    ins = e.matmul(out, l, r, start=(first and i == 0), stop=(last and i == n - 1))
                return ins
            return P.op("pe", fn, reads, writes)

        def act(out, in_, func, reads, writes, bias=None, scale=None):
            kw = {}
            if bias is not None:
                kw["bias"] = bias
            if scale is not None:
                kw["scale"] = scale
            return P.op("act", lambda e, out=out, in_=in_, func=func, kw=kw: e.activation(out=out, in_=in_, func=func, **kw),
                        reads, writes)

        def ts(eng, out, in0, s1, s2, op0, op1, reads, writes):
            if s2 is None:
                return P.op(eng, lambda e: e.tensor_scalar(out, in0, s1, None, op0), reads, writes)
            return P.op(eng, lambda e: e.tensor_scalar(out, in0, s1, s2, op0, op1), reads, writes)

        def tt(eng, out, in0, in1, op, reads, writes):
            return P.op(eng, lambda e: e.tensor_tensor(out, in0, in1, op), reads, writes)

        def stt(eng, out, in0, scalar, in1, op0, op1, reads, writes):
            return P.op(eng, lambda e: e.scalar_tensor_tensor(out, in0, scalar, in1, op0, op1), reads, writes)

        cb_b = buf("cb")
        sm_b = buf("smalls")
        idf_b = buf("identf")
        mod_b = buf("modT")
        vec_b = buf("vecs")

        def cbs(o, n=128, rows=slice(0, 128)):
            return cbt[rows, o:o + n]

        P.op("dve", lambda e: e.memset(epsc[:, :], EPS), (), (sm_b,))
        dma("sp", cbt[:, :], cb_d[:, :], (), (cb_b,))
        dma("sp", smalls[:, :], smalls_d[:, :], (), (sm_b,))
        dma("sp", identf[:, :], identf_d[:, :], (), (idf_b,))
        V_b = buf("V")

        stage = big[:, :, :].rearrange("p a b -> p (a b)").bitcast(F32)
        stage_b = [BG(0, 8192), BG(8192, 16384)]
        xsrc = [("ctx", ctx_d, 0, 2)] + [("lat", x_d, g * 4, 4) for g in range(4)]
        for gi, (seg, src, t0, ntile) in enumerate(xsrc):
            sl = gi % 2
            st = stage[:, sl * 4096:(sl + 1) * 4096].rearrange("p (t d) -> p t d", t=4)
            dma("sp", st[:, 0:ntile, :], src[t0 * 128:(t0 + ntile) * 128, :].rearrange("(t p) d -> p t d", p=128),
                (), (stage_b[sl],))
            for k in range(8):
                bk, bkb = bka()
                def fn(e, bk=bk, st=st, k=k, ntile=ntile):
                    ins = None
                    for j in range(ntile):
                        ins = e.transpose(bk[:, j * 128:(j + 1) * 128], st[:, j, k * 128:(k + 1) * 128], identf[:, :])
                    return ins
                P.op("pe", fn, (stage_b[sl], idf_b), (bkb,))
                W = ntile * 128
                c0 = 0 if seg == "ctx" else CTX + t0 * 128
                act(resid(seg, k, c0, W), bk[:, 0:W], AF.Identity, (bkb,), (rb(seg, k, c0),))

        o_cc = _off["cc"][0]
        scb_b = buf("scb")
        act(scb[:, :, :].rearrange("p k s -> p (k s)"), smalls[:, o_cc:o_cc + 16], AF.Silu, (sm_b,), (scb_b,))
        mod_loaded = []

        def mod_load(l, blk):
            rg, rgb = ringa()
            dma("pool", rg, wmod_d[l, blk].rearrange("p k c -> p (k c)"), (), (rgb,))
            mod_loaded.append((l, blk, rg, rgb))

        def mod_step(bank_idx=None):
            while len(mod_loaded) < 3 and pending_mod:
                mod_load(*pending_mod.pop(0))
            if mod_loaded:
                l_, blk_, rg, rgb = mod_loaded.pop(0)
                mod_block(l_, blk_, bank_idx, rg, rgb)

        def mod_block(l, blk, bank_idx=None, rg=None, rgb=None):
            if rg is None:
                rg, rgb = ringa()
                dma("pool", rg, wmod_d[l, blk].rearrange("p k c -> p (k c)"), (), (rgb,))
            rg3 = rg.rearrange("p (k c) -> p k c", k=8)
            if bank_idx is None:
                bk, bkb = bka()
            else:
                bk, bkb = banks[bank_idx], bank_b[bank_idx]
            for half in range(2):
                mm(bk[:, half * 2:half * 2 + 2],
                   [(rg3[:, k, half * 128:(half + 1) * 128], scb[:, k, :]) for k in range(8)],
                   (rgb, scb_b), (bkb,))
            o_b = _off["bmod"][0] + l * 48 + blk * 2
            for half in range(2):
                ts("dve", modT[:, l, blk * 2 + half, :], bk[:, half * 2:half * 2 + 2],
                   smalls[:, o_b + half:o_b + half + 1], None, ALU.add, None, (bkb, sm_b), (mod_b,))

        for blk in range(8):
            mod_block(0, blk)
        pending_mod = [(0, blk) for blk in range(8, 24)] + [(l_, blk) for l_ in range(1, DEPTH) for blk in range(24)]

        ones_ap = cbs(CB_ONES)
        bones_ap = cbs(CB_BONES)
        rot_ap = cbs(CB_ROT)
        idb_ap = cbs(CB_ID)

        def layer_vecs(l, part):
            if part == 0:
                dma("pool", bdpm[:, :, :].rearrange("p a b -> p (a b)"), bdpm_d[:, l * 256:(l + 1) * 256], (), (buf("bdpm"),))
                for s in range(2):
                    o_n = _off["norm1"][0] + l * 8
                    stt("dve", vecs[:, s, 0, :], modT[:, l, 8:16, s], 1.0, smalls[:, o_n:o_n + 8], ALU.add, ALU.mult,
                        (mod_b, sm_b), (vec_b,))
                    P.op("dve", lambda e, s=s: e.tensor_copy(vecs[:, s, 1, :], modT[:, l, 0:8, s]), (mod_b,), (vec_b,))
                return
            for s in range(2):
                P.op("dve", lambda e, s=s: e.tensor_copy(vecs[:, s, 2, :], modT[:, l, 16:24, s]), (mod_b,), (vec_b,))
                o_n = _off["norm2"][0] + l * 8
                stt("dve", vecs[:, s, 3, :], modT[:, l, 32:40, s], 1.0, smalls[:, o_n:o_n + 8], ALU.add, ALU.mult,
                    (mod_b, sm_b), (vec_b,))
                P.op("dve", lambda e, s=s: e.tensor_copy(vecs[:, s, 4, :], modT[:, l, 24:32, s]), (mod_b,), (vec_b,))
                P.op("dve", lambda e, s=s: e.tensor_copy(vecs[:, s, 5, :], modT[:, l, 40:48, s]), (mod_b,), (vec_b,))
            return
            for s in range(2):
                for half, nname in ((0, "norm1"), (1, "norm2")):
                    j0 = half * 3
                    o_n = _off[nname][0] + l * 8
                    stt("dve", vecs[:, s, j0 + 0, :], modT[:, l, (j0 + 1) * 8:(j0 + 2) * 8, s], 1.0,
                        smalls[:, o_n:o_n + 8], ALU.add, ALU.mult, (mod_b, sm_b), (vec_b,))
                    P.op("dve", lambda e, s=s, j0=j0: e.tensor_copy(vecs[:, s, j0 + 1, :], modT[:, l, (j0 + 0) * 8:(j0 + 1) * 8, s]),
                         (mod_b,), (vec_b,))
                    P.op("dve", lambda e, s=s, j0=j0: e.tensor_copy(vecs[:, s, j0 + 2, :], modT[:, l, (j0 + 2) * 8:(j0 + 3) * 8, s]),
                         (mod_b,), (vec_b,))
            dma("pool", bdpm[:, :, :].rearrange("p a b -> p (a b)"), bdpm_d[:, l * 256:(l + 1) * 256], (), (buf("bdpm"),))

        def rstd_from(ps_ap, psb, W):
            r, rbf = rsa()
            act(r[:, 0:W], ps_ap, AF.Ln, (psb, sm_b), (rbf,), bias=epsc[:, 0:1])
            act(r[:, 0:W], r[:, 0:W], AF.Exp, (rbf,), (rbf,), scale=-0.5)
            return r, rbf

        def norm_phase(seg, which):
            s = 0 if seg == "lat" else 1
            j0 = which * 3
            for (sg, c0, W) in tiles_of(seg):
                bk, bkb = bka()
                for k in range(8):
                    sq, sqb = tba()
                    act(sq[:, 0:W], resid(seg, k, c0, W), AF.Square, (rb(seg, k, c0),), (sqb,))
                    mm(bk[:, 0:W], [(ones_ap, sq[:, 0:W])], (sqb, cb_b), (bkb,), first=(k == 0), last=(k == 7))
                r, rbf = rstd_from(bk[:, 0:W], bkb, W)
                for k in range(8):
                    t, tbf = tfa()
                    stt("dve", t[:, 0:W], resid(seg, k, c0, W), vecs[:, s, j0 + 0, k:k + 1], r[:, 0:W], ALU.mult, ALU.mult,
                        (rb(seg, k, c0), rbf, vec_b), (tbf,))
                    ts("dve", hT[:, k, c0:c0 + W], t[:, 0:W], vecs[:, s, j0 + 1, k:k + 1], None, ALU.add, None,
                       (tbf, vec_b), (hb(c0),))

        def load_w(src_ap):
            rg, rgb = ringa()
            dma("pool", rg, src_ap, (), (rgb,))
            return rg, rgb

        def proj_fm(rg3, rgb, half, c0, W):
            bk, bkb = bka()
            mm(bk[:, 0:W], [(rg3[:, k, half * 128:(half + 1) * 128], hT[:, k, c0:c0 + W]) for k in range(8)],
               (rgb, hb(c0)), (bkb,))
            return bk, bkb

        def headnorm_rope(bk, bkb, seg, c0, W, l, which, dst_ap, dst_b):
            o_g = _off["qkg"][0] + l * 2 + which
            if "hnA" in FLAGS:
                ts("dve", dst_ap, bk[:, 0:W], smalls[:, o_g:o_g + 1], None, ALU.mult, None, (bkb, sm_b), (dst_b,))
                return
            if "hnB" in FLAGS:
                sq, sqb = tba()
                act(sq[:, 0:W], bk[:, 0:W], AF.Square, (bkb,), (sqb,))
                act(dst_ap, bk[:, 0:W], AF.Identity, (bkb,), (dst_b,))
                return
            sq, sqb = tba()
            act(sq[:, 0:W], bk[:, 0:W], AF.Square, (bkb,), (sqb,))
            y, yb = tba()
            ts("dve", y[:, 0:W], bk[:, 0:W], smalls[:, o_g:o_g + 1], None, ALU.mult, None, (bkb, sm_b), (yb,))
            b2, b2b = bka()
            mm(b2[:, 0:W], [(bones_ap, sq[:, 0:W])], (sqb, cb_b), (b2b,))
            r, rbf = rstd_from(b2[:, 0:W], b2b, W)
            if seg == "lat" and "norope" not in FLAGS:
                t = (c0 - CTX) // 512
                rp, rpb = ma()
                rp3 = rp.rearrange("p (a b) -> p a b", a=2)
                dma("sp", rp3, rope_d[t], (), (rpb,))
                b3, b3b = bka()
                mm(b3[:, 0:W], [(rot_ap, y[:, 0:W])], (yb, cb_b), (b3b,))
                t1, t1b = tfa()
                tt("dve", t1[:, 0:W], y[:, 0:W], rp3[:, 0, 0:W], ALU.mult, (yb, rpb), (t1b,))
                t2, t2b = tfa()
                tt("dve", t2[:, 0:W], b3[:, 0:W], rp3[:, 1, 0:W], ALU.mult, (b3b, rpb), (t2b,))
                tt("pool", t1[:, 0:W], t1[:, 0:W], t2[:, 0:W], ALU.add, (t1b, t2b), (t1b,))
                tt("dve", dst_ap, t1[:, 0:W], r[:, 0:W], ALU.mult, (t1b, rbf), (dst_b,))
            elif "hnC" in FLAGS:
                P.op("dve", lambda e: e.tensor_copy(dst_ap, y[:, 0:W]), (yb, rbf), (dst_b,))
            elif "hnD" in FLAGS:
                t1, t1b = tfa()
                tt("dve", t1[:, 0:W], y[:, 0:W], r[:, 0:W], ALU.mult, (yb, rbf), (t1b,))
                P.op("dve", lambda e: e.tensor_copy(dst_ap, t1[:, 0:W]), (t1b,), (dst_b,))
            else:
                t1, t1b = tfa()
                P.op("dve", lambda e: e.tensor_copy(t1[:, 0:W], y[:, 0:W]), (yb,), (t1b,))
                tt("dve", dst_ap, t1[:, 0:W], r[:, 0:W], ALU.mult, (t1b, rbf), (dst_b,))

        def conv3(eng, dst, src, w_ap3, segs, reads, writes):
            for (c0, n) in segs:
                ts(eng, dst[:, c0:c0 + n], src[:, c0:c0 + n], w_ap3[:, 1:2], None, ALU.mult, None, reads, writes)
                stt(eng, dst[:, c0 + 1:c0 + n], src[:, c0:c0 + n - 1], w_ap3[:, 0:1], dst[:, c0 + 1:c0 + n],
                    ALU.mult, ALU.add, (reads, writes), writes)
                stt(eng, dst[:, c0:c0 + n - 1], src[:, c0 + 1:c0 + n], w_ap3[:, 2:3], dst[:, c0:c0 + n - 1],
                    ALU.mult, ALU.add, (reads, writes), writes)

        def ck(l, n):
            if limit is not None and limit == l * 10 + n:
                raise _Stop()

        def layer(l):
            last = (l == DEPTH - 1)
            full_segs = ["lat"] if last else ["ctx", "lat"]
            layer_vecs(l, 0)
            for seg in ("ctx", "lat"):
                norm_phase(seg, 0)
            ck(l, 1)
            all_tiles = tiles_of("ctx") + tiles_of("lat")
            full_tiles = [t for t in all_tiles if t[0] in full_segs]
            seglist = [SEGS[s] for s in full_segs]

            rg, rgb = load_w(win_d[l, 0].rearrange("p k c -> p (k c)"))
            rg3 = rg.rearrange("p (k c) -> p k c", k=8)
            KT_b = {}
            for (seg, c0, W) in ([] if "nok" in FLAGS else all_tiles):
                bk, bkb = proj_fm(rg3, rgb, 0, c0, W)
                kb = KTB(c0)
                headnorm_rope(bk, bkb, seg, c0, W, l, 1, KT[:, c0:c0 + W], kb)
            for g in range(0 if "nov" in FLAGS else 5):
                nt_ = 4 if g < 4 else 2
                bk, bkb = bka()
                for j in range(nt_):
                    i = g * 4 + j
                    c0t = (i * 128) // 512 * 512 if i >= 2 else 0
                    c0t = 0 if i < 2 else CTX + ((i - 2) // 4) * 512
                    mm(bk[:, j * 128:(j + 1) * 128],
                       [(hT[:, k, i * 128:(i + 1) * 128], rg3[:, k, 128:256]) for k in range(8)],
                       (rgb, hb(c0t)), (bkb,))
                act(Vt[:, g * 4:g * 4 + nt_, :], bk[:, 0:nt_ * 128].rearrange("p (t c) -> p t c", t=nt_), AF.Identity,
                    (bkb,), (V_b,))

            ck(l, 2)
            if True:
                rg, rgb = load_w(win_d[l, 1].rearrange("p k c -> p (k c)"))
                rg3 = rg.rearrange("p (k c) -> p k c", k=8)
                for half in range(2):
                    for (seg, c0, W) in full_tiles:
                        bk, bkb = proj_fm(rg3, rgb, half, c0, W)
                        act(big[:, 6 + half, c0:c0 + W], bk[:, 0:W], AF.Identity, (bkb,), (bigb(6 + half, c0),))
                xcs_all = big[:, 0:4, :].rearrange("p a b -> p (a b)").rearrange("p (t k s c) -> p t k s c", t=18, k=2, s=2)
                bdcs = cbs(CB_BDCS, 256)
                for seg in full_segs:
                    s0, n = SEGS[seg]
                    ntl = n // 128
                    t_base = s0 // 128
                    for i in range(ntl):
                        ti = t_base + i
                        c0t = 0 if seg == "ctx" else CTX + (i // 4) * 512
                        bk, bkb = bka()
                        def fnx(e, bk=bk, ti=ti):
                            ins = None
                            for k in range(2):
                                ins = e.matmul(bk[:, k * 256:(k + 1) * 256], big[:, 6 + k, ti * 128:(ti + 1) * 128], bdcs,
                                               start=True, stop=True)
                            return ins
                        P.op("pe", fnx, (bigb(6, c0t), bigb(7, c0t), cb_b), (bkb,))
                        act(xcs_all[:, ti, :, :, :].rearrange("p k s c -> p (k s c)"), bk[:, 0:512], AF.Identity,
                            (bkb,), (BG(ti * 512, ti * 512 + 512),))
                    for (sg, c0, W) in tiles_of(seg):
                        j = 0 if seg == "ctx" else (c0 - CTX) // 512
                        bks = [bka(), bka()]
                        for i in range(ntl):
                            ti = t_base + i
                            dsl, dsb = ma()
                            d3 = dsl.rearrange("p (a b) -> p a b", a=2)
                            if seg == "ctx":
                                dma("sp", d3[:, :, 0:256], dftc_d[i], (), (dsb,))
                            else:
                                dma("sp", d3, dftl_d[j, i], (), (dsb,))
                            def fny(e, bks=bks, ti=ti, d3=d3, W=W, i=i, ntl=ntl):
                                ins = None
                                for k in range(2):
                                    for cs in range(2):
                                        ins = e.matmul(bks[k][0][:, 0:W], xcs_all[:, ti, k, cs, :], d3[:, cs, 0:W],
                                                       start=(i == 0 and cs == 0), stop=(i == ntl - 1 and cs == 1))
                                return ins
                            P.op("pe", fny, (BG(ti * 512, ti * 512 + 512), dsb), (bks[0][1], bks[1][1]))
                        for k in range(2):
                            act(big[:, 6 + k, c0:c0 + W], bks[k][0][:, 0:W], AF.Identity, (bks[k][1],), (bigb(6 + k, c0),))

                ck(l, 3)
                sc_blocks = {}
                chunk_slot = {}
                for ci in range(4, 10):
                    blk, half = ci // 2, ci % 2
                    if blk not in sc_blocks:
                        rg, rgb = load_w(win_d[l, blk].rearrange("p k c -> p (k c)"))
                        sc_blocks[blk] = (rg.rearrange("p (k c) -> p k c", k=8), rgb)
                    rg3, rgb = sc_blocks[blk]
                    j = (ci - 4) // 3
                    part = (ci - 4) % 3
                    for (seg, c0, W) in full_tiles:
                        bk, bkb = proj_fm(rg3, rgb, half, c0, W)
                        act(big[:, part, c0:c0 + W], bk[:, 0:W], AF.Identity, (bkb,), (bigb(part, c0),))
                    if part == 2:
                        o_c = _off["convsc"][0] + (l * 2 + j) * 3
                        allb = lambda ch: tuple(bigb(ch, c0) for (_, c0, _) in full_tiles)
                        lo = seglist[0][0]
                        hi = seglist[-1][0] + seglist[-1][1]
                        tt("dve", big[:, 3, lo:hi], big[:, 1, lo:hi], big[:, 2, lo:hi], ALU.mult,
                           allb(1) + allb(2), allb(3))
                        conv3("dve", big[:, 1, :], big[:, 3, :], smalls[:, o_c:o_c + 3], seglist, allb(3) + (sm_b,), allb(1))
                        tt("dve", big[:, 4 + j, lo:hi], big[:, 0, lo:hi], big[:, 1, lo:hi], ALU.mult,
                           allb(0) + allb(1), allb(4 + j))

                ck(l, 4)
                rg, rgb = load_w(win_d[l, 5].rearrange("p k c -> p (k c)"))
                rg3 = rg.rearrange("p (k c) -> p k c", k=8)
                xp = big[:, 0:2, :].rearrange("p a b -> p (a b)").rearrange("p (t c) -> p t c", t=18)
                for seg in full_segs:
                    s0, n = SEGS[seg]
                    for i0 in range(s0 // 128, (s0 + n) // 128, 2):
                        bk, bkb = bka()
                        c0t = 0 if seg == "ctx" else CTX + ((i0 - 2) // 4) * 512
                        for j in range(2):
                            i = i0 + j
                            mm(bk[:, j * 256:(j + 1) * 256],
                               [(hT[:, k, i * 128:(i + 1) * 128], rg3[:, k, 0:256]) for k in range(8)],
                               (rgb, hb(c0t)), (bkb,))
                        act(xp[:, i0:i0 + 2, :].rearrange("p t c -> p (t c)"), bk[:, 0:512], AF.Identity, (bkb,),
                            (BG(i0 * 256, i0 * 256 + 512),))
                for seg in full_segs:
                    s0, n = SEGS[seg]
                    t_lo, t_hi = s0 // 128, (s0 + n) // 128
                    for ch in range(2):
                        for i0 in range(t_lo, t_hi, 2):
                            bk, bkb = bka()
                            rd = set()
                            for jj in range(2):
                                i = i0 + jj
                                for g in range(2):
                                    wi = 2 * ch + g
                                    pairs = []
                                    for dlt, v in ((-1, 0), (0, 1), (1, 2)):
                                        ii = i + dlt
                                        if ii < t_lo or ii >= t_hi:
                                            continue
                                        vv = v
                                        if dlt == 0 and i == t_lo:
                                            vv = 3
                                        if dlt == 0 and i == t_hi - 1:
                                            vv = 4
                                        pairs.append((xp[:, ii, ch * 128:(ch + 1) * 128],
                                                      cbs(CB_BAND + (wi * 5 + vv) * 128)))
                                        rd.update(BG(ii * 256, ii * 256 + 256))
                                    mm(bk[:, (jj * 2 + g) * 128:(jj * 2 + g + 1) * 128], pairs, tuple(rd) + (cb_b,), (bkb,))
                            c0t = 0 if seg == "ctx" else CTX + ((i0 - 2) // 4) * 512
                            bk4 = bk[:, :].rearrange("p (j g n) -> p j g n", j=2, g=2)
                            pdst = big[:, 2 + ch, i0 * 128:(i0 + 2) * 128].rearrange("p (j n) -> p j n", j=2)
                            pb_ = BG((2 + ch) * NT + i0 * 128, (2 + ch) * NT + i0 * 128 + 256)
                            P.op("act", lambda e, pdst=pdst, bk4=bk4: e.activation(out=pdst[0:64], in_=bk4[0:64, :, 0, :], func=AF.Identity),
                                 (bkb,), (pb_,))
                            P.op("dve", lambda e, pdst=pdst, bk4=bk4: e.tensor_copy(pdst[64:128], bk4[64:128, :, 1, :]),
                                 (bkb,), (pb_,))
                    for ch in range(2):
                        o_s = _off["pscale"][0] + l * 2 + ch
                        for (sg, c0, W) in tiles_of(seg):
                            bk, bkb = bka()
                            rd = BG((2 + ch) * NT + c0, (2 + ch) * NT + c0 + W)
                            mm(bk[:, 0:W], [(bdpm[:, ch, :], big[:, 2 + ch, c0:c0 + W])], rd + (buf("bdpm"),), (bkb,))
                            act(pbuf[:, ch, c0:c0 + W], bk[:, 0:W], AF.Identity, (bkb, sm_b), (pbb(ch, c0),),
                                scale=smalls[:, o_s:o_s + 1])

                ck(l, 5)
                for blk in (6, 7):
                    rg, rgb = load_w(win_d[l, blk].rearrange("p k c -> p (k c)"))
                    rg3 = rg.rearrange("p (k c) -> p k c", k=8)
                    for half in range(2):
                        j = (blk - 6) * 2 + half
                        for (seg, c0, W) in full_tiles:
                            bk, bkb = proj_fm(rg3, rgb, half, c0, W)
                            headnorm_rope(bk, bkb, seg, c0, W, l, 0, big[:, j, c0:c0 + W], bigb(j, c0))

                ck(l, 6)
                SP = [(0, 1), (2, 3)]
                AA, AB, DA, DB = 4, 5, 6, 7
                one1 = cbs(CB_ONE)
                pairs = []
                chunks = []
                for seg in full_segs:
                    kts = [0, 1] if seg == "ctx" else list(range(2, 18)) + [0, 1]
                    for (sg, c0, W) in tiles_of(seg):
                        for j in range(4):
                            ci = len(chunks)
                            chunks.append((seg, c0, W, j))
                            for ki, kt in enumerate(kts):
                                pairs.append((ci, kt, ki == 0, ki == len(kts) - 1))

                def emit_S(pi):
                    ci, kt, first, last_ = pairs[pi]
                    seg, c0, W, j = chunks[ci]
                    kc0 = 0 if kt < 2 else CTX + ((kt - 2) // 4) * 512
                    for hh in range(2):
                        rows = slice(hh * 64, (hh + 1) * 64)
                        b_ = SP[pi % 2][hh]
                        mm(banks[b_][:, 0:W], [(KT[rows, kt * 128:(kt + 1) * 128], big[rows, j, c0:c0 + W])],
                           (KTB(kc0), bigb(j, c0)), (bank_b[b_],))

                emit_S(0)
                for pi in range(len(pairs)):
                    if pi + 1 < len(pairs):
                        emit_S(pi + 1)
                    ci, kt, first, last_ = pairs[pi]
                    seg, c0, W, j = chunks[ci]
                    pts = []
                    for hh in range(2):
                        b_ = SP[pi % 2][hh]
                        pt, ptb = pta()
                        act(pt[:, 0:W], banks[b_][:, 0:W], AF.Exp, (bank_b[b_],), (ptb,), scale=0.125)
                        pts.append((pt, ptb))
                    for hh in range(2):
                        pt, ptb = pts[hh]
                        ab = (AA, AB)[hh]
                        db = (DA, DB)[hh]
                        mm(banks[ab][:, 0:W], [(Vt[:, kt, :], pt[:, 0:W])], (ptb, V_b), (bank_b[ab],), first=first, last=last_)
                        mm(banks[db][:, 0:W], [(one1, pt[:, 0:W])], (ptb, cb_b), (bank_b[db],), first=first, last=last_)
                    if (pending_mod or mod_loaded) and pi % 7 == 6:
                        mod_step(SP[pi % 2][0])
                    if last_:
                        Tn, Tnb = rsa()
                        Td, Tdb = tfa()
                        P.op("dve", lambda e, Tn=Tn, W=W: e.tensor_copy(Tn[0:64, 0:W], banks[AA][0:64, 0:W]), (bank_b[AA],), (Tnb,))
                        P.op("dve", lambda e, Td=Td, W=W: e.tensor_copy(Td[0:64, 0:W], banks[DA][0:64, 0:W]), (bank_b[DA],), (Tdb,))
                        act(Tn[64:128, 0:W], banks[AB][64:128, 0:W], AF.Identity, (bank_b[AB],), (Tnb,))
                        act(Td[64:128, 0:W], banks[DB][64:128, 0:W], AF.Identity, (bank_b[DB],), (Tdb,))
                        P.op("dve", lambda e, Td=Td, W=W: e.reciprocal(Td[:, 0:W], Td[:, 0:W]), (Tdb,), (Tdb,))
                        tt("pool", Tn[:, 0:W], Tn[:, 0:W], Td[:, 0:W], ALU.mult, (Tnb, Tdb), (Tnb,))
                        P.op("pool", lambda e, Tn=Tn, j=j, c0=c0, W=W: e.tensor_copy(big[:, j, c0:c0 + W], Tn[:, 0:W]),
                             (Tnb,), (bigb(j, c0),))

                while pending_mod or mod_loaded:
                    mod_step()
                layer_vecs(l, 1)
                ck(l, 7)
                brin = [(big, 0), (big, 1), (big, 2), (big, 3), (big, 4), (big, 5), (big, 6), (big, 7), (pbuf, 0), (pbuf, 1)]
                broff = [(0, 4), (4, 2), (6, 2), (8, 2)]

                def brb(kk, c0):
                    t_, ch = brin[kk]
                    return bigb(ch, c0) if t_ is big else pbb(ch, c0)

                for c in range(8):
                    g0, g0b = load_w(win_d[l, 8 + 2 * c].rearrange("p k c -> p (k c)"))
                    g1, g1b = load_w(win_d[l, 9 + 2 * c].rearrange("p k c -> p (k c)"))
                    gsl = [(g0.rearrange("p (k c) -> p k c", k=8), g0b, 0), (g0.rearrange("p (k c) -> p k c", k=8), g0b, 1),
                           (g1.rearrange("p (k c) -> p k c", k=8), g1b, 0), (g1.rearrange("p (k c) -> p k c", k=8), g1b, 1)]
                    wb, wbb = ringa()
                    dma("pool", wb[:, 0:1280], wbr_d[l, c].rearrange("p k c -> p (k c)"), (), (wbb,))
                    wb3 = wb[:, 0:1280].rearrange("p (k c) -> p k c", k=10)
                    wo, wob = ringa()
                    dma("pool", wo[:, 0:1024], wout_d[l, c], (), (wob,))
                    def stage_b(pv):
                        seg, c0, W, yt, ytb = pv
                        s_ = 0 if seg == "lat" else 1
                        for dch in range(8):
                            bw, bwb = bka()
                            mm(bw[:, 0:W], [(wo[:, dch * 128:(dch + 1) * 128], yt[:, 0:W])], (wob, ytb), (bwb,))
                            r_ap = resid(seg, dch, c0, W)
                            stt("dve", r_ap, bw[:, 0:W], vecs[:, s_, 2, dch:dch + 1], r_ap, ALU.mult, ALU.add,
                                (bwb, vec_b, rb(seg, dch, c0)), (rb(seg, dch, c0),))

                    prev = None
                    for ti_, (seg, c0, W) in enumerate(full_tiles):
                        acc, accb = rsa()
                        yi = (c * len(full_tiles) + ti_) % 2
                        yt, ytb = ytp[:, yi, :], ytp_b[yi]
                        for i in range(4):
                            rg3, rgb, half = gsl[i]
                            bk, bkb = proj_fm(rg3, rgb, half, c0, W)
                            gt, gtb = tba()
                            act(gt[:, 0:W], bk[:, 0:W], AF.Sigmoid, (bkb,), (gtb,))
                            k0, nk = broff[i]
                            bo, bob = bka()
                            pairs_ = []
                            rd = []
                            for kk in range(k0, k0 + nk):
                                t_, ch = brin[kk]
                                pairs_.append((wb3[:, kk, :], t_[:, ch, c0:c0 + W]))
                                rd.append(brb(kk, c0))
                            mm(bo[:, 0:W], pairs_, tuple(rd) + (wbb,), (bob,))
                            if i == 0:
                                tt("dve", acc[:, 0:W], bo[:, 0:W], gt[:, 0:W], ALU.mult, (bob, gtb), (accb,))
                            else:
                                t2, t2b = tfa()
                                tt("dve", t2[:, 0:W], bo[:, 0:W], gt[:, 0:W], ALU.mult, (bob, gtb), (t2b,))
                                if i < 3:
                                    tt("pool", acc[:, 0:W], acc[:, 0:W], t2[:, 0:W], ALU.add, (accb, t2b), (accb,))
                                else:
                                    tt("pool", yt[:, 0:W], acc[:, 0:W], t2[:, 0:W], ALU.add, (accb, t2b), (ytb,))
                        if prev is not None:
                            stage_b(prev)
                        prev = (seg, c0, W, yt, ytb)
                    stage_b(prev)

            ck(l, 8)
            for seg in full_segs:
                norm_phase(seg, 1)
            ck(l, 8.5)
            CA = KTt
            CB = Vt2
            lo = seglist[0][0]
            hi = seglist[-1][0] + seglist[-1][1]
            ua_b = tuple(pbb(0, c0) for (_, c0, _) in full_tiles)
            ub_b = tuple(pbb(1, c0) for (_, c0, _) in full_tiles)
            ca_b = KT_ALL
            cb2_b = V_b

            def taps(dst, dstb, src, srcb, o_w):
                w0 = smalls[:, o_w:o_w + 1]
                w2 = smalls[:, o_w + 2:o_w + 3]
                for (s0, n) in seglist:
                    stt("dve", dst[:, s0 + 2:s0 + n + 1], src[:, s0:s0 + n - 1], w0, dst[:, s0 + 2:s0 + n + 1],
                        ALU.mult, ALU.add, (srcb, sm_b, dstb), (dstb,))
                    stt("dve", dst[:, s0 + 2:s0 + n], src[:, s0 + 2:s0 + n], w2, dst[:, s0 + 2:s0 + n],
                        ALU.mult, ALU.add, (srcb, sm_b, dstb), (dstb,))
                    stt("dve", dst[:, s0 + 1:s0 + 2], src[:, s0 + 1:s0 + 2], w2, dst[:, s0 + 1:s0 + 2],
                        ALU.mult, ALU.add, (srcb, sm_b, dstb), (dstb,))

            def down_proj(w3, njj):
                for (seg, c0, W) in full_tiles:
                    s = 0 if seg == "lat" else 1
                    for dch in range(8):
                        bw, bwb = bka()
                        mm(bw[:, 0:W], [(w3[:, jj, dch * 128:(dch + 1) * 128], bigflat[:, jj * NT + c0 + 1:jj * NT + c0 + 1 + W])
                                        for jj in range(njj)],
                           tuple(BG(jj * NT + c0 + 1, jj * NT + c0 + 1 + W) for jj in range(njj)) + (wsb_of[id(w3)],), (bwb,))
                        r_ap = resid(seg, dch, c0, W)
                        stt("dve", r_ap, bw[:, 0:W], vecs[:, s, 5, dch:dch + 1], r_ap, ALU.mult, ALU.add,
                            (bwb, vec_b, rb(seg, dch, c0)), (rb(seg, dch, c0),))

            wsb_of = {}
            pendingD = None
            for g in range(NGRP):
                njj = min(GSZ, NFF - g * GSZ)
                wo_ = (4 + 2 * (g % 2)) * NT + 2
                wslot = bigflat[:, wo_:wo_ + GSZ * 1024]
                wsb = BG(wo_, wo_ + GSZ * 1024)
                dma("pool", wslot, wdn_d[l, g].rearrange("p j c -> p (j c)"), (), (wsb,))
                w3 = wslot.rearrange("p (j c) -> p j c", j=GSZ)
                wsb_of[id(w3)] = wsb
                for jj in range(njj):
                    j = g * GSZ + jj
                    o_a = _off["wcf"][0] + (l * 44 + j) * 3
                    o_b = _off["wcf"][0] + (l * 44 + 22 + j) * 3
                    rg, rgb = load_w(wup_d[l, j].rearrange("p k c -> p (k c)"))
                    rg3 = rg.rearrange("p (k c) -> p k c", k=8)
                    for (seg, c0, W) in full_tiles:
                        bk, bkb = proj_fm(rg3, rgb, 0, c0, W)
                        act(pbuf[:, 0, c0:c0 + W], bk[:, 0:W], AF.Identity, (bkb,), (pbb(0, c0),))
                        act(CA[:, c0 + 1:c0 + 1 + W], bk[:, 0:W], AF.Identity, (bkb, sm_b), (ca_b,),
                            scale=smalls[:, o_a + 1:o_a + 2])
                        bk, bkb = proj_fm(rg3, rgb, 1, c0, W)
                        act(pbuf[:, 1, c0:c0 + W], bk[:, 0:W], AF.Identity, (bkb,), (pbb(1, c0),))
                        act(CB[:, c0 + 1:c0 + 1 + W], bk[:, 0:W], AF.Identity, (bkb, sm_b), (cb2_b,),
                            scale=smalls[:, o_b + 1:o_b + 2])
                    if jj == 0 and pendingD is not None:
                        down_proj(*pendingD)
                        pendingD = None
                    taps(CA, ca_b, pbuf[:, 0, :], ua_b, o_a)
                    taps(CB, cb2_b, pbuf[:, 1, :], ub_b, o_b)
                    act(CA[:, lo + 1:hi + 1], CA[:, lo + 1:hi + 1], AF.Silu, (ca_b,), (ca_b,))
                    tt("pool", bigflat[:, jj * NT + lo + 1:jj * NT + hi + 1], CA[:, lo + 1:hi + 1], CB[:, lo + 1:hi + 1], ALU.mult,
                       (ca_b, cb2_b), (BG(jj * NT + lo + 1, jj * NT + hi + 1),))
                pendingD = (w3, njj)
            down_proj(*pendingD)
            ck(l, 9)

        try:
            if limit == 0:
                raise _Stop()
            for l in range(DEPTH):
                layer(l)
        except _Stop:
            pass

        out_ops = []
        if dbg_d is not None:
            o_ = 0
            for di, (fn_, n_) in enumerate(dbg[0]):
                out_ops.append(dma("sp", dbg_d[:, o_:o_ + n_], fn_(locals()), tuple(B.values()), (buf(f"dbgout{di}"),) + tuple(B.values())))
                o_ += n_
        ostage = big[:, :, :].rearrange("p a b -> p (a b)").bitcast(F32)
        o_f = _off["fnorm"][0]
        for ti, (seg, c0, W) in enumerate(tiles_of("lat")):
            bk, bkb = bka()
            for k in range(8):
                sq, sqb = tba()
                act(sq[:, 0:W], resid(seg, k, c0, W), AF.Square, (rb(seg, k, c0),), (sqb,))
                mm(bk[:, 0:W], [(ones_ap, sq[:, 0:W])], (sqb, cb_b), (bkb,), first=(k == 0), last=(k == 7))
            r, rbf = rstd_from(bk[:, 0:W], bkb, W)
            osl = ti % 2
            ost = ostage[:, osl * 4096:(osl + 1) * 4096].rearrange("p (s d) -> p s d", s=4)
            osb = BG(osl * 8192, osl * 8192 + 8192)
            for k in range(8):
                t, tbf = tfa()
                stt("dve", t[:, 0:W], resid(seg, k, c0, W), smalls[:, o_f + k:o_f + k + 1], r[:, 0:W], ALU.mult, ALU.mult,
                    (rb(seg, k, c0), rbf, sm_b), (tbf,))
                b2, b2b = bka()
                def fnO(e, t=t, b2=b2):
                    ins = None
                    for sidx in range(4):
                        ins = e.transpose(b2[:, sidx * 128:(sidx + 1) * 128], t[:, sidx * 128:(sidx + 1) * 128], identf[:, :])
                    return ins
                P.op("pe", fnO, (tbf, idf_b), (b2b,))
                act(ost[:, :, k * 128:(k + 1) * 128], b2[:, 0:512].rearrange("p (s d) -> p s d", s=4), AF.Identity,
                    (b2b,), (osb,))
            t0 = (c0 - CTX)
            out_ops.append(dma("sp", out_d[t0:t0 + 512, :].rearrange("(s p) d -> p s d", p=128), ost, (osb,), (buf(f"outd{ti}"),)))

        eng_sem = {e: es.enter_context(nc.semaphore(f"sem_{e}")) for e in Prog.ENGS}
        NDS = 8
        dma_sems = {q: [es.enter_context(nc.semaphore(f"dsem_{q}{i}")) for i in range(NDS)] for q in ("sp", "pool")}
        for e in Prog.ENGS:
            c_ = 0
            nd = 0
            for o in P.ops[e]:
                if o.is_dma:
                    o.sem = dma_sems[e][nd % NDS]
                    o.val = 16 * (nd // NDS + 1)
                    o.prev_val = 16 * (nd // NDS)
                    nd += 1
                elif o.needed:
                    c_ += 1
                    o.sem = eng_sem[e]
                    o.val = c_
        handles = {"pe": None, "act": None, "dve": None, "pool": None, "sp": None}

        def emit(ename, h):
            waited = {}
            for o in P.ops[ename]:
                need = {}
                for d in o.deps:
                    k = id(d.sem)
                    if k not in need or need[k][1] < d.val:
                        need[k] = (d.sem, d.val)
                if o.is_dma and o.prev_val > 0:
                    k = id(o.sem)
                    if k not in need or need[k][1] < o.prev_val:
                        need[k] = (o.sem, o.prev_val)
                for k, (s_, v_) in need.items():
                    if waited.get(k, 0) < v_:
                        h.wait_ge(s_, v_)
                        waited[k] = v_
                ins = o.fn(h)
                if o.is_dma:
                    ins.then_inc(o.sem, 16)
                elif o.needed:
                    ins.then_inc(o.sem, 1)
            if ename == "sp":
                for o in out_ops:
                    if waited.get(id(o.sem), 0) < o.val:
                        h.wait_ge(o.sem, o.val)
                        waited[id(o.sem)] = o.val

        with nc.Block() as block:
            @block.sync
            def _(e):
                emit("sp", e)

            @block.tensor
            def _(e):
                emit("pe", e)

            @block.scalar
            def _(e):
                emit("act", e)

            @block.vector
            def _(e):
                emit("dve", e)

            @block.gpsimd
            def _(e):
                emit("pool", e)
    return nc


_CACHE = {}


def kernel(**inputs):
    sh = _prep_shared(inputs)
    xs = np.asarray(inputs["x"], dtype=np.float32)
    cs = np.asarray(inputs["ctx"], dtype=np.float32)
    in_maps = []
    for b in range(8):
        m = dict(sh)
        m["x"] = np.ascontiguousarray(xs[b])
        m["ctx"] = np.ascontiguousarray(cs[b])
        m["smalls"], m["bdpm"] = _smalls(inputs, b)
        in_maps.append(m)
    nc = build_nc()
    res = run_bass_kernel_spmd(nc, in_maps, core_ids=list(range(8)))
    out = np.stack([np.asarray(res.results[b]["out"], dtype=np.float32) for b in range(8)], axis=0)
    return out
```

```python
import contextlib
import numpy as np
import ml_dtypes
import concourse.bass as bass
import concourse.mybir as mybir
from concourse.bass_utils import run_bass_kernel_spmd

F32 = mybir.dt.float32
BF16 = mybir.dt.bfloat16
AF = mybir.ActivationFunctionType
ALU = mybir.AluOpType

D = 1024
SEQ = 2048
CTX = 256
NT = SEQ + CTX
DEPTH = 2
DFF = 2816
NFF = 22
EPS = 1e-6
GSZ = 4
NGRP = 6

_off = {}
_ns = 0


def _alloc(name, n):
    global _ns
    _off[name] = (_ns, n)
    _ns += n


_alloc("cc", 16)
_alloc("bmod", 2 * 48)
_alloc("norm1", 16)
_alloc("norm2", 16)
_alloc("fnorm", 8)
_alloc("convsc", 2 * 2 * 3)
_alloc("qkg", 2 * 2)
_alloc("pscale", 2 * 2)
_alloc("wcf", 2 * 44 * 3)
NS = _ns

CB_ID = 0
CB_ONES = 128
CB_BONES = 256
CB_ROT = 384
CB_BDCS = 512
CB_BAND = 768
CB_ONE = CB_BAND + 20 * 128
NCB = CB_ONE + 128

OFF_Q, OFF_K, OFF_V, OFF_SC, OFF_F, OFF_P, OFF_G = 0, 512, 640, 768, 1536, 1792, 2048


def _win_cols():
    cols = []
    cols += list(range(OFF_K, OFF_K + 128))
    cols += list(range(OFF_V, OFF_V + 128))
    cols += list(range(OFF_F, OFF_F + 256))
    for j in range(2):
        for part in range(3):
            s = OFF_SC + part * 256 + j * 128
            cols += list(range(s, s + 128))
    cols += list(range(OFF_P, OFF_P + 256))
    for j in range(4):
        cols += list(range(OFF_Q + j * 64, OFF_Q + j * 64 + 64))
        cols += list(range(OFF_Q + (4 + j) * 64, OFF_Q + (4 + j) * 64 + 64))
    for c in range(8):
        for i in range(4):
            s = OFF_G + i * 1024 + c * 128
            cols += list(range(s, s + 128))
    return np.array(cols)


def _blk(w, cb):
    K, C = w.shape
    return np.ascontiguousarray(w.reshape(K // 128, 128, C // cb, cb).transpose(2, 1, 0, 3))


def _bf(a):
    return np.ascontiguousarray(a.astype(ml_dtypes.bfloat16))


def _consts():
    cb = np.zeros((128, NCB), np.float32)
    cb[:, CB_ID:CB_ID + 128] = np.eye(128)
    cb[:, CB_ONES:CB_ONES + 128] = 1.0 / 1024
    for h in range(2):
        cb[h * 64:(h + 1) * 64, CB_BONES + h * 64:CB_BONES + (h + 1) * 64] = 1.0 / 64
    cb[:, CB_ONE:CB_ONE + 128] = 1.0
    for p in range(128):
        half = (p // 16) % 2
        if half == 0:
            cb[p + 16, CB_ROT + p] = -1.0
        else:
            cb[p - 16, CB_ROT + p] = 1.0
    cidx = np.arange(64)
    ph = 2 * np.pi * np.outer(cidx, cidx) / 64.0
    for g in range(2):
        cb[g * 64:(g + 1) * 64, CB_BDCS + g * 64:CB_BDCS + (g + 1) * 64] = np.cos(ph)
        cb[g * 64:(g + 1) * 64, CB_BDCS + 128 + g * 64:CB_BDCS + 128 + (g + 1) * 64] = np.sin(ph)
    n = 384
    t = np.arange(n)
    for wi, w in enumerate((2, 4, 8, 16)):
        left, right = (w - 1) // 2, w // 2
        lo = np.maximum(t - left, 0)
        hi = np.minimum(t + right + 1, n)
        P = np.zeros((n, n), np.float64)
        for q in range(n):
            P[lo[q]:hi[q], q] = 1.0 / (hi[q] - lo[q])
            P[q, q] -= 1.0
        mats = [P[0:128, 128:256], P[128:256, 128:256], P[256:384, 128:256], P[0:128, 0:128], P[256:384, 256:384]]
        for v, m in enumerate(mats):
            o = CB_BAND + (wi * 5 + v) * 128
            cb[:, o:o + 128] = m
    inv = 10000.0 ** (-np.arange(16, dtype=np.float32) / 16)
    pos = np.arange(SEQ)
    row = (pos // 64).astype(np.float32)
    col = (pos % 64).astype(np.float32)
    p = np.arange(128)
    d = p % 64
    axis = d // 32
    pair = d % 16
    posax = np.where(axis[:, None] == 0, row[None, :], col[None, :]).astype(np.float32)
    ang = (posax * inv[pair][:, None]).astype(np.float32)
    rope = np.stack([np.cos(ang), np.sin(ang)], axis=1)
    rope = rope.reshape(128, 2, 4, 512).transpose(2, 0, 1, 3)
    def dft(N, wj):
        nn = np.arange(N, dtype=np.int64)
        prod = (np.outer(nn, nn) % N).astype(np.float64) * (2 * np.pi / N)
        sc = 1.0 / np.sqrt(64.0 * N)
        C = np.cos(prod) * sc
        S = -np.sin(prod) * sc
        T = np.stack([C, S], axis=1)
        T = T.reshape(N // 128, 128, 2, N // wj, wj).transpose(3, 0, 1, 2, 4)
        return T
    dftl = dft(SEQ, 512)
    dftc = dft(CTX, 256)[0]
    return _bf(cb), _bf(rope), _bf(dftl), _bf(dftc)


def _prep_shared(inp):
    f = lambda a: np.asarray(a, dtype=np.float32)
    cols = _win_cols()
    sh = {}
    sh["wmod"] = np.stack([_blk(f(inp["w_mod"][l]), 256) for l in range(DEPTH)])
    sh["win"] = np.stack([_blk(f(inp["w_in"][l])[:, cols], 256) for l in range(DEPTH)])
    hperm = []
    for j in range(4):
        hperm += list(range(j * 64, j * 64 + 64)) + list(range((4 + j) * 64, (4 + j) * 64 + 64))
    wbr = []
    for l in range(DEPTH):
        w = np.concatenate([f(inp["w_br_attn"][l])[hperm], f(inp["w_br_sc"][l]), f(inp["w_br_f"][l]),
                            f(inp["w_br_p"][l])], axis=0)
        wbr.append(_blk(w, 128))
    sh["wbr"] = np.stack(wbr)
    sh["wout"] = np.ascontiguousarray(f(inp["w_out"]).reshape(DEPTH, 8, 128, 1024))
    wup = []
    for l in range(DEPTH):
        w = f(inp["w_up"][l])
        a = w[:, :DFF].reshape(D, NFF, 128)
        b = w[:, DFF:].reshape(D, NFF, 128)
        ab = np.concatenate([a, b], axis=2).reshape(D, NFF * 256)
        wup.append(_blk(ab, 256))
    sh["wup"] = np.stack(wup)
    wdn = np.zeros((DEPTH, NGRP, 128, GSZ, 1024), np.float32)
    for l in range(DEPTH):
        w = f(inp["w_down"][l]).reshape(NFF, 128, 1024)
        for j in range(NFF):
            wdn[l, j // GSZ, :, j % GSZ, :] = w[j]
    sh["wdn"] = wdn
    cb, rope, dftl, dftc = _consts()
    sh["cb"] = cb
    sh["rope"] = rope
    sh["dftl"] = dftl
    sh["dftc"] = dftc
    sh["identf"] = np.eye(128, dtype=np.float32)
    return sh


def _smalls(inp, b):
    f = lambda a: np.asarray(a, dtype=np.float32)
    s = np.zeros((128, NS), np.float32)

    def put(name, arr):
        o, n = _off[name]
        s[:, o:o + n] = arr.reshape(128, n)

    cc = np.stack([f(inp["c"])[b].reshape(8, 128).T, f(inp["c_ctx"]).reshape(8, 128).T], axis=2)
    put("cc", cc)
    put("bmod", f(inp["b_mod"]).reshape(DEPTH, 48, 128).transpose(2, 0, 1))
    put("norm1", f(inp["norm1"]).reshape(DEPTH, 8, 128).transpose(2, 0, 1))
    put("norm2", f(inp["norm2"]).reshape(DEPTH, 8, 128).transpose(2, 0, 1))
    put("fnorm", f(inp["final_norm"]).reshape(8, 128).T)
    put("convsc", f(inp["conv_sc"]).reshape(DEPTH, 3, 2, 128).transpose(3, 0, 2, 1))
    qk = f(inp["qk_gain"])
    put("qkg", np.tile(qk.transpose(2, 0, 1), (2, 1, 1)))
    put("pscale", f(inp["pool_scale"]).reshape(DEPTH, 2, 128).transpose(2, 0, 1))
    put("wcf", f(inp["w_conv_ffn"]).reshape(DEPTH, 3, 44, 128).transpose(3, 0, 2, 1))
    pm = f(inp["pool_mat"])
    bd = np.zeros((128, DEPTH, 2, 128), np.float32)
    for l in range(DEPTH):
        for ch in range(2):
            for g in range(2):
                bd[g * 64:(g + 1) * 64, l, ch, g * 64:(g + 1) * 64] = pm[l, 2 * ch + g]
    return s, np.ascontiguousarray(bd.reshape(128, DEPTH * 256))


def _flat(x):
    out = []
    st = [x]
    while st:
        a = st.pop()
        if isinstance(a, (tuple, list, set)):
            st.extend(a)
        else:
            out.append(a)
    return out


class Buf:
    __slots__ = ("name", "w", "r", "excl")

    def __init__(self, name, excl=False):
        self.name = name
        self.w = None
        self.r = []
        self.excl = excl


class Op:
    __slots__ = ("eng", "fn", "deps", "is_dma", "sem", "val", "needed", "prev_val")


class Prog:
    ENGS = ("pe", "act", "dve", "pool", "sp")

    def __init__(self):
        self.ops = {e: [] for e in self.ENGS}

    def op(self, eng, fn, reads=(), writes=(), dma=False):
        o = Op()
        o.eng = eng
        o.fn = fn
        o.is_dma = dma
        o.needed = False
        o.sem = None
        o.val = 0
        o.prev_val = 0
        reads = _flat(reads)
        writes = _flat(writes)
        ex = [b for b in reads if b.excl]
        if ex:
            reads = [b for b in reads if not b.excl]
            writes = writes + ex
        deps = {}
        for b in reads:
            if b.w is not None:
                deps[id(b.w)] = b.w
        for b in writes:
            if b.w is not None:
                deps[id(b.w)] = b.w
            for r in b.r:
                deps[id(r)] = r
        dl = []
        for d in deps.values():
            if d is o:
                continue
            if d.eng == eng and not d.is_dma and eng == "pe":
                continue
            d.needed = True
            dl.append(d)
        o.deps = dl
        for b in reads:
            b.r.append(o)
        for b in writes:
            b.w = o
            b.r = []
        self.ops[eng].append(o)
        return o


FLAGS = set()


class _Stop(Exception):
    pass


def build_nc(dbg=None, limit=None):
    nc = bass.Bass("TRN2", target_bir_lowering=False)
    dram = {}

    def din(name, shape, dt=F32):
        dram[name] = nc.dram_tensor(name, list(shape), dt, kind="ExternalInput").ap()
        return dram[name]

    x_d = din("x", [SEQ, D])
    ctx_d = din("ctx", [CTX, D])
    smalls_d = din("smalls", [128, NS])
    bdpm_d = din("bdpm", [128, DEPTH * 256])
    cb_d = din("cb", [128, NCB], BF16)
    identf_d = din("identf", [128, 128])
    rope_d = din("rope", [4, 128, 2, 512], BF16)
    dftl_d = din("dftl", [4, 16, 128, 2, 512], BF16)
    dftc_d = din("dftc", [2, 128, 2, 256], BF16)
    wmod_d = din("wmod", [DEPTH, 24, 128, 8, 256])
    win_d = din("win", [DEPTH, 24, 128, 8, 256])
    wbr_d = din("wbr", [DEPTH, 8, 128, 10, 128])
    wout_d = din("wout", [DEPTH, 8, 128, 1024])
    wup_d = din("wup", [DEPTH, NFF, 128, 8, 256])
    wdn_d = din("wdn", [DEPTH, NGRP, 128, GSZ, 1024])
    out_d = nc.dram_tensor("out", [SEQ, D], F32, kind="ExternalOutput").ap()
    dbg_d = None
    if dbg is not None:
        dbg_d = nc.dram_tensor("dbg", [128, sum(n for _, n in dbg[0])], dbg[1], kind="ExternalOutput").ap()

    P = Prog()
    es = contextlib.ExitStack()
    with es:
        def sb(name, shape, dt):
            return es.enter_context(nc.sbuf_tensor(name, list(shape), dt))

        latT = sb("latT", [128, 8, SEQ], F32)
        cxT = sb("cxT", [128, 8, CTX], F32)
        hT = sb("hT", [128, 8, NT], BF16)
        big = sb("big", [128, 8, NT], BF16)
        pbuf = sb("pbuf", [128, 2, NT], BF16)
        KTt = sb("KTt", [128, NT], BF16)
        Vt = sb("Vt", [128, 18, 128], BF16)
        cbt = sb("cbt", [128, NCB], BF16)
        identf = sb("identf_sb", [128, 128], F32)
        smalls = sb("smalls_sb", [128, NS], F32)
        modT = sb("modT", [128, DEPTH, 48, 2], F32)
        vecs = sb("vecs", [128, 2, 6, 8], F32)
        bdpm = sb("bdpm_bf", [128, 2, 128], BF16)
        scb = sb("scb", [128, 8, 2], BF16)
        NRING = 4
        ring = sb("ring", [128, NRING, 2048], BF16)
        NTF = 2
        tf = sb("tf", [128, NTF, 512], F32)
        NRS = 2
        rs = sb("rs", [128, NRS, 512], F32)
        NTB = 2
        tb = sb("tb", [128, NTB, 512], BF16)
        NM = 3
        mreg = sb("mreg", [128, NM, 1024], BF16)
        small1 = sb("small1", [128, 8], F32)
        ytp = sb("ytp", [128, 2, 512], BF16)
        ytp_b = [Buf("ytp0"), Buf("ytp1")]
        epsc = sb("epsc", [128, 1], F32)
        banks = [es.enter_context(nc.psum_tensor(f"ps{i}", [128, 512], F32)) for i in range(8)]

        KT = KTt[:, :]
        Vflat = Vt[:, :, :].rearrange("p t c -> p (t c)")

        B = {}

        def buf(name):
            if name not in B:
                B[name] = Buf(name)
            return B[name]

        bank_b = [Buf(f"bank{i}", True) for i in range(8)]
        ring_b = [Buf(f"ring{i}") for i in range(NRING)]
        tf_b = [Buf(f"tf{i}") for i in range(NTF)]
        rs_b = [Buf(f"rs{i}") for i in range(NRS)]
        tb_b = [Buf(f"tb{i}") for i in range(NTB)]
        mh_b = [Buf(f"mh{i}") for i in range(2 * NM)]
        m_b = [(mh_b[2 * i], mh_b[2 * i + 1]) for i in range(NM)]
        cnt = {"ring": 0, "tf": 0, "tb": 0, "m": 0, "bank": 0, "rs": 0, "pt": 0}

        def nxt(kind, n):
            i = cnt[kind] % n
            cnt[kind] += 1
            return i

        def tfa():
            i = nxt("tf", NTF)
            return tf[:, i, :], tf_b[i]

        def rsa():
            i = nxt("rs", NRS)
            return rs[:, i, :], rs_b[i]

        def tba():
            i = nxt("tb", NTB)
            return tb[:, i, :], tb_b[i]

        def ma():
            i = nxt("m", NM)
            return mreg[:, i, :], m_b[i]

        def pta():
            i = nxt("pt", 2 * NM)
            return mreg[:, i // 2, (i % 2) * 512:(i % 2) * 512 + 512], mh_b[i]

        def bka(pool=None):
            pool = pool if pool is not None else list(range(8))
            i = pool[cnt["bank"] % len(pool)]
            cnt["bank"] += 1
            return banks[i], bank_b[i]

        def ringa():
            i = nxt("ring", NRING)
            return ring[:, i, :], ring_b[i]

        SEGS = {"ctx": (0, CTX), "lat": (CTX, SEQ)}

        def tiles_of(seg):
            s0, n = SEGS[seg]
            if seg == "ctx":
                return [("ctx", 0, 256)]
            return [("lat", s0 + 512 * t, 512) for t in range(4)]

        def resid(seg, k, c0, W):
            if seg == "ctx":
                return cxT[:, k, c0:c0 + W]
            return latT[:, k, c0 - CTX:c0 - CTX + W]

        def rb(seg, k, c0):
            return buf(f"res_{seg}_{k}_{c0}")

        def hb(c0):
            return buf(f"hT_{c0}")

        def BG(a, b):
            return tuple(buf(f"bigf_{i}") for i in range(a // 256, (b - 1) // 256 + 1))

        def bigb(ch, c0):
            W_ = 256 if c0 == 0 else 512
            return BG(ch * NT + c0, ch * NT + c0 + W_)

        def KTB(c0):
            return buf(f"KT_{c0}")

        KT_ALL = tuple(KTB(c0) for c0 in (0, 256, 768, 1280, 1792))

        def pbb(ch, c0):
            return buf(f"pb_{ch}_{c0}")

        def sm(name, *idx):
            o, n = _off[name]
            return o

        def dma(eng, out, in_, reads=(), writes=()):
            h = {"sp": nc.sync, "pool": nc.gpsimd}[eng]
            return P.op(eng, lambda e, out=out, in_=in_: e.dma_start(out=out, in_=in_), reads, writes, dma=True)

        def mm(out, pairs, reads, writes, first=True, last=True):
            def fn(e, out=out, pairs=pairs, first=first, last=last):
                ins = None
                n = len(pairs)
                for i, (l, r) in enumerate(pairs):
                    ins = e.matmul(out, l, r, start=(first and i == 0), stop=(last and i == n - 1))
                return ins
            return P.op("pe", fn, reads, writes)

        def act(out, in_, func, reads, writes, bias=None, scale=None):
            kw = {}
            if bias is not None:
                kw["bias"] = bias
            if scale is not None:
                kw["scale"] = scale
            return P.op("act", lambda e, out=out, in_=in_, func=func, kw=kw: e.activation(out=out, in_=in_, func=func, **kw),
                        reads, writes)

        def ts(eng, out, in0, s1, s2, op0, op1, reads, writes):
            if s2 is None:
                return P.op(eng, lambda e: e.tensor_scalar(out, in0, s1, None, op0), reads, writes)
            return P.op(eng, lambda e: e.tensor_scalar(out, in0, s1, s2, op0, op1), reads, writes)

        def tt(eng, out, in0, in1, op, reads, writes):
            return P.op(eng, lambda e: e.tensor_tensor(out, in0, in1, op), reads, writes)

        def stt(eng, out, in0, scalar, in1, op0, op1, reads, writes):
            return P.op(eng, lambda e: e.scalar_tensor_tensor(out, in0, scalar, in1, op0, op1), reads, writes)

        cb_b = buf("cb")
        sm_b = buf("smalls")
        idf_b = buf("identf")
        mod_b = buf("modT")
        vec_b = buf("vecs")

        def cbs(o, n=128, rows=slice(0, 128)):
            return cbt[rows, o:o + n]

        P.op("dve", lambda e: e.memset(epsc[:, :], EPS), (), (sm_b,))
        dma("sp", cbt[:, :], cb_d[:, :], (), (cb_b,))
        dma("sp", smalls[:, :], smalls_d[:, :], (), (sm_b,))
        dma("sp", identf[:, :], identf_d[:, :], (), (idf_b,))
        V_b = buf("V")

        stage = big[:, :, :].rearrange("p a b -> p (a b)").bitcast(F32)
        stage_b = [BG(0, 8192), BG(8192, 16384)]
        xsrc = [("ctx", ctx_d, 0, 2)] + [("lat", x_d, g * 4, 4) for g in range(4)]
        for gi, (seg, src, t0, ntile) in enumerate(xsrc):
            sl = gi % 2
            st = stage[:, sl * 4096:(sl + 1) * 4096].rearrange("p (t d) -> p t d", t=4)
            dma("sp", st[:, 0:ntile, :], src[t0 * 128:(t0 + ntile) * 128, :].rearrange("(t p) d -> p t d", p=128),
                (), (stage_b[sl],))
            for k in range(8):
                bk, bkb = bka()
                def fn(e, bk=bk, st=st, k=k, ntile=ntile):
                    ins = None
                    for j in range(ntile):
                        ins = e.transpose(bk[:, j * 128:(j + 1) * 128], st[:, j, k * 128:(k + 1) * 128], identf[:, :])
                    return ins
                P.op("pe", fn, (stage_b[sl], idf_b), (bkb,))
                W = ntile * 128
                c0 = 0 if seg == "ctx" else CTX + t0 * 128
                act(resid(seg, k, c0, W), bk[:, 0:W], AF.Identity, (bkb,), (rb(seg, k, c0),))

        o_cc = _off["cc"][0]
        scb_b = buf("scb")
        act(scb[:, :, :].rearrange("p k s -> p (k s)"), smalls[:, o_cc:o_cc + 16], AF.Silu, (sm_b,), (scb_b,))
        mod_loaded = []

        def mod_load(l, blk):
            rg, rgb = ringa()
            dma("pool", rg, wmod_d[l, blk].rearrange("p k c -> p (k c)"), (), (rgb,))
            mod_loaded.append((l, blk, rg, rgb))

        def mod_step(bank_idx=None):
            while len(mod_loaded) < 3 and pending_mod:
                mod_load(*pending_mod.pop(0))
            if mod_loaded:
                l_, blk_, rg, rgb = mod_loaded.pop(0)
                mod_block(l_, blk_, bank_idx, rg, rgb)

        def mod_block(l, blk, bank_idx=None, rg=None, rgb=None):
            if rg is None:
                rg, rgb = ringa()
                dma("pool", rg, wmod_d[l, blk].rearrange("p k c -> p (k c)"), (), (rgb,))
            rg3 = rg.rearrange("p (k c) -> p k c", k=8)
            if bank_idx is None:
                bk, bkb = bka()
            else:
                bk, bkb = banks[bank_idx], bank_b[bank_idx]
            for half in range(2):
                mm(bk[:, half * 2:half * 2 + 2],
                   [(rg3[:, k, half * 128:(half + 1) * 128], scb[:, k, :]) for k in range(8)],
                   (rgb, scb_b), (bkb,))
            o_b = _off["bmod"][0] + l * 48 + blk * 2
            for half in range(2):
                ts("dve", modT[:, l, blk * 2 + half, :], bk[:, half * 2:half * 2 + 2],
                   smalls[:, o_b + half:o_b + half + 1], None, ALU.add, None, (bkb, sm_b), (mod_b,))

        for blk in range(8):
            mod_block(0, blk)
        pending_mod = [(0, blk) for blk in range(8, 24)] + [(l_, blk) for l_ in range(1, DEPTH) for blk in range(24)]

        ones_ap = cbs(CB_ONES)
        bones_ap = cbs(CB_BONES)
        rot_ap = cbs(CB_ROT)
        idb_ap = cbs(CB_ID)

        def layer_vecs(l, part):
            if part == 0:
                dma("pool", bdpm[:, :, :].rearrange("p a b -> p (a b)"), bdpm_d[:, l * 256:(l + 1) * 256], (), (buf("bdpm"),))
                for s in range(2):
                    o_n = _off["norm1"][0] + l * 8
                    stt("dve", vecs[:, s, 0, :], modT[:, l, 8:16, s], 1.0, smalls[:, o_n:o_n + 8], ALU.add, ALU.mult,
                        (mod_b, sm_b), (vec_b,))
                    P.op("dve", lambda e, s=s: e.tensor_copy(vecs[:, s, 1, :], modT[:, l, 0:8, s]), (mod_b,), (vec_b,))
                return
            for s in range(2):
                P.op("dve", lambda e, s=s: e.tensor_copy(vecs[:, s, 2, :], modT[:, l, 16:24, s]), (mod_b,), (vec_b,))
                o_n = _off["norm2"][0] + l * 8
                stt("dve", vecs[:, s, 3, :], modT[:, l, 32:40, s], 1.0, smalls[:, o_n:o_n + 8], ALU.add, ALU.mult,
                    (mod_b, sm_b), (vec_b,))
                P.op("dve", lambda e, s=s: e.tensor_copy(vecs[:, s, 4, :], modT[:, l, 24:32, s]), (mod_b,), (vec_b,))
                P.op("dve", lambda e, s=s: e.tensor_copy(vecs[:, s, 5, :], modT[:, l, 40:48, s]), (mod_b,), (vec_b,))
            return
            for s in range(2):
                for half, nname in ((0, "norm1"), (1, "norm2")):
                    j0 = half * 3
                    o_n = _off[nname][0] + l * 8
                    stt("dve", vecs[:, s, j0 + 0, :], modT[:, l, (j0 + 1) * 8:(j0 + 2) * 8, s], 1.0,
                        smalls[:, o_n:o_n + 8], ALU.add, ALU.mult, (mod_b, sm_b), (vec_b,))
                    P.op("dve", lambda e, s=s, j0=j0: e.tensor_copy(vecs[:, s, j0 + 1, :], modT[:, l, (j0 + 0) * 8:(j0 + 1) * 8, s]),
                         (mod_b,), (vec_b,))
                    P.op("dve", lambda e, s=s, j0=j0: e.tensor_copy(vecs[:, s, j0 + 2, :], modT[:, l, (j0 + 2) * 8:(j0 + 3) * 8, s]),
                         (mod_b,), (vec_b,))
            dma("pool", bdpm[:, :, :].rearrange("p a b -> p (a b)"), bdpm_d[:, l * 256:(l + 1) * 256], (), (buf("bdpm"),))

        def rstd_from(ps_ap, psb, W):
            r, rbf = rsa()
            act(r[:, 0:W], ps_ap, AF.Ln, (psb, sm_b), (rbf,), bias=epsc[:, 0:1])
            act(r[:, 0:W], r[:, 0:W], AF.Exp, (rbf,), (rbf,), scale=-0.5)
            return r, rbf

        def norm_phase(seg, which):
            s = 0 if seg == "lat" else 1
            j0 = which * 3
            for (sg, c0, W) in tiles_of(seg):
                bk, bkb = bka()
                for k in range(8):
                    sq, sqb = tba()
                    act(sq[:, 0:W], resid(seg, k, c0, W), AF.Square, (rb(seg, k, c0),), (sqb,))
                    mm(bk[:, 0:W], [(ones_ap, sq[:, 0:W])], (sqb, cb_b), (bkb,), first=(k == 0), last=(k == 7))
                r, rbf = rstd_from(bk[:, 0:W], bkb, W)
                for k in range(8):
                    t, tbf = tfa()
                    stt("dve", t[:, 0:W], resid(seg, k, c0, W), vecs[:, s, j0 + 0, k:k + 1], r[:, 0:W], ALU.mult, ALU.mult,
                        (rb(seg, k, c0), rbf, vec_b), (tbf,))
                    ts("dve", hT[:, k, c0:c0 + W], t[:, 0:W], vecs[:, s, j0 + 1, k:k + 1], None, ALU.add, None,
                       (tbf, vec_b), (hb(c0),))

        def load_w(src_ap):
            rg, rgb = ringa()
            dma("pool", rg, src_ap, (), (rgb,))
            return rg, rgb

        def proj_fm(rg3, rgb, half, c0, W):
            bk, bkb = bka()
            mm(bk[:, 0:W], [(rg3[:, k, half * 128:(half + 1) * 128], hT[:, k, c0:c0 + W]) for k in range(8)],
               (rgb, hb(c0)), (bkb,))
            return bk, bkb

        def headnorm_rope(bk, bkb, seg, c0, W, l, which, dst_ap, dst_b):
            o_g = _off["qkg"][0] + l * 2 + which
            if "hnA" in FLAGS:
                ts("dve", dst_ap, bk[:, 0:W], smalls[:, o_g:o_g + 1], None, ALU.mult, None, (bkb, sm_b), (dst_b,))
                return
            if "hnB" in FLAGS:
                sq, sqb = tba()
                act(sq[:, 0:W], bk[:, 0:W], AF.Square, (bkb,), (sqb,))
                act(dst_ap, bk[:, 0:W], AF.Identity, (bkb,), (dst_b,))
                return
            sq, sqb = tba()
            act(sq[:, 0:W], bk[:, 0:W], AF.Square, (bkb,), (sqb,))
            y, yb = tba()
            ts("dve", y[:, 0:W], bk[:, 0:W], smalls[:, o_g:o_g + 1], None, ALU.mult, None, (bkb, sm_b), (yb,))
            b2, b2b = bka()
            mm(b2[:, 0:W], [(bones_ap, sq[:, 0:W])], (sqb, cb_b), (b2b,))
            r, rbf = rstd_from(b2[:, 0:W], b2b, W)
            if seg == "lat" and "norope" not in FLAGS:
                t = (c0 - CTX) // 512
                rp, rpb = ma()
                rp3 = rp.rearrange("p (a b) -> p a b", a=2)
                dma("sp", rp3, rope_d[t], (), (rpb,))
                b3, b3b = bka()
                mm(b3[:, 0:W], [(rot_ap, y[:, 0:W])], (yb, cb_b), (b3b,))
                t1, t1b = tfa()
                tt("dve", t1[:, 0:W], y[:, 0:W], rp3[:, 0, 0:W], ALU.mult, (yb, rpb), (t1b,))
                t2, t2b = tfa()
                tt("dve", t2[:, 0:W], b3[:, 0:W], rp3[:, 1, 0:W], ALU.mult, (b3b, rpb), (t2b,))
                tt("pool", t1[:, 0:W], t1[:, 0:W], t2[:, 0:W], ALU.add, (t1b, t2b), (t1b,))
                tt("dve", dst_ap, t1[:, 0:W], r[:, 0:W], ALU.mult, (t1b, rbf), (dst_b,))
            elif "hnC" in FLAGS:
                P.op("dve", lambda e: e.tensor_copy(dst_ap, y[:, 0:W]), (yb, rbf), (dst_b,))
            elif "hnD" in FLAGS:
                t1, t1b = tfa()
                tt("dve", t1[:, 0:W], y[:, 0:W], r[:, 0:W], ALU.mult, (yb, rbf), (t1b,))
                P.op("dve", lambda e: e.tensor_copy(dst_ap, t1[:, 0:W]), (t1b,), (dst_b,))
            else:
                t1, t1b = tfa()
                P.op("dve", lambda e: e.tensor_copy(t1[:, 0:W], y[:, 0:W]), (yb,), (t1b,))
                tt("dve", dst_ap, t1[:, 0:W], r[:, 0:W], ALU.mult, (t1b, rbf), (dst_b,))

        def conv3(eng, dst, src, w_ap3, segs, reads, writes):
            for (c0, n) in segs:
                ts(eng, dst[:, c0:c0 + n], src[:, c0:c0 + n], w_ap3[:, 1:2], None, ALU.mult, None, reads, writes)
                stt(eng, dst[:, c0 + 1:c0 + n], src[:, c0:c0 + n - 1], w_ap3[:, 0:1], dst[:, c0 + 1:c0 + n],
                    ALU.mult, ALU.add, (reads, writes), writes)
                stt(eng, dst[:, c0:c0 + n - 1], src[:, c0 + 1:c0 + n], w_ap3[:, 2:3], dst[:, c0:c0 + n - 1],
                    ALU.mult, ALU.add, (reads, writes), writes)

        def ck(l, n):
            if limit is not None and limit == l * 10 + n:
                raise _Stop()

        def layer(l):
            last = (l == DEPTH - 1)
            full_segs = ["lat"] if last else ["ctx", "lat"]
            layer_vecs(l, 0)
            for seg in ("ctx", "lat"):
                norm_phase(seg, 0)
            ck(l, 1)
            all_tiles = tiles_of("ctx") + tiles_of("lat")
            full_tiles = [t for t in all_tiles if t[0] in full_segs]
            seglist = [SEGS[s] for s in full_segs]

            rg, rgb = load_w(win_d[l, 0].rearrange("p k c -> p (k c)"))
            rg3 = rg.rearrange("p (k c) -> p k c", k=8)
            KT_b = {}
            for (seg, c0, W) in ([] if "nok" in FLAGS else all_tiles):
                bk, bkb = proj_fm(rg3, rgb, 0, c0, W)
                kb = KTB(c0)
                headnorm_rope(bk, bkb, seg, c0, W, l, 1, KT[:, c0:c0 + W], kb)
            for g in range(0 if "nov" in FLAGS else 5):
                nt_ = 4 if g < 4 else 2
                bk, bkb = bka()
                for j in range(nt_):
                    i = g * 4 + j
                    c0t = (i * 128) // 512 * 512 if i >= 2 else 0
                    c0t = 0 if i < 2 else CTX + ((i - 2) // 4) * 512
                    mm(bk[:, j * 128:(j + 1) * 128],
                       [(hT[:, k, i * 128:(i + 1) * 128], rg3[:, k, 128:256]) for k in range(8)],
                       (rgb, hb(c0t)), (bkb,))
                act(Vt[:, g * 4:g * 4 + nt_, :], bk[:, 0:nt_ * 128].rearrange("p (t c) -> p t c", t=nt_), AF.Identity,
                    (bkb,), (V_b,))

            ck(l, 2)
            if True:
                rg, rgb = load_w(win_d[l, 1].rearrange("p k c -> p (k c)"))
                rg3 = rg.rearrange("p (k c) -> p k c", k=8)
                for half in range(2):
                    for (seg, c0, W) in full_tiles:
                        bk, bkb = proj_fm(rg3, rgb, half, c0, W)
                        act(big[:, 6 + half, c0:c0 + W], bk[:, 0:W], AF.Identity, (bkb,), (bigb(6 + half, c0),))
                xcs_all = big[:, 0:4, :].rearrange("p a b -> p (a b)").rearrange("p (t k s c) -> p t k s c", t=18, k=2, s=2)
                bdcs = cbs(CB_BDCS, 256)
                for seg in full_segs:
                    s0, n = SEGS[seg]
                    ntl = n // 128
                    t_base = s0 // 128
                    for i in range(ntl):
                        ti = t_base + i
                        c0t = 0 if seg == "ctx" else CTX + (i // 4) * 512
                        bk, bkb = bka()
                        def fnx(e, bk=bk, ti=ti):
                            ins = None
                            for k in range(2):
                                ins = e.matmul(bk[:, k * 256:(k + 1) * 256], big[:, 6 + k, ti * 128:(ti + 1) * 128], bdcs,
                                               start=True, stop=True)
                            return ins
                        P.op("pe", fnx, (bigb(6, c0t), bigb(7, c0t), cb_b), (bkb,))
                        act(xcs_all[:, ti, :, :, :].rearrange("p k s c -> p (k s c)"), bk[:, 0:512], AF.Identity,
                            (bkb,), (BG(ti * 512, ti * 512 + 512),))
                    for (sg, c0, W) in tiles_of(seg):
                        j = 0 if seg == "ctx" else (c0 - CTX) // 512
                        bks = [bka(), bka()]
                        for i in range(ntl):
                            ti = t_base + i
                            dsl, dsb = ma()
                            d3 = dsl.rearrange("p (a b) -> p a b", a=2)
                            if seg == "ctx":
                                dma("sp", d3[:, :, 0:256], dftc_d[i], (), (dsb,))
                            else:
                                dma("sp", d3, dftl_d[j, i], (), (dsb,))
                            def fny(e, bks=bks, ti=ti, d3=d3, W=W, i=i, ntl=ntl):
                                ins = None
                                for k in range(2):
                                    for cs in range(2):
                                        ins = e.matmul(bks[k][0][:, 0:W], xcs_all[:, ti, k, cs, :], d3[:, cs, 0:W],
                                                       start=(i == 0 and cs == 0), stop=(i == ntl - 1 and cs == 1))
                                return ins
                            P.op("pe", fny, (BG(ti * 512, ti * 512 + 512), dsb), (bks[0][1], bks[1][1]))
                        for k in range(2):
                            act(big[:, 6 + k, c0:c0 + W], bks[k][0][:, 0:W], AF.Identity, (bks[k][1],), (bigb(6 + k, c0),))

                ck(l, 3)
                sc_blocks = {}
                chunk_slot = {}
                for ci in range(4, 10):
                    blk, half = ci // 2, ci % 2
                    if blk not in sc_blocks:
                        rg, rgb = load_w(win_d[l, blk].rearrange("p k c -> p (k c)"))
                        sc_blocks[blk] = (rg.rearrange("p (k c) -> p k c", k=8), rgb)
                    rg3, rgb = sc_blocks[blk]
                    j = (ci - 4) // 3
                    part = (ci - 4) % 3
                    for (seg, c0, W) in full_tiles:
                        bk, bkb = proj_fm(rg3, rgb, half, c0, W)
                        act(big[:, part, c0:c0 + W], bk[:, 0:W], AF.Identity, (bkb,), (bigb(part, c0),))
                    if part == 2:
                        o_c = _off["convsc"][0] + (l * 2 + j) * 3
                        allb = lambda ch: tuple(bigb(ch, c0) for (_, c0, _) in full_tiles)
                        lo = seglist[0][0]
                        hi = seglist[-1][0] + seglist[-1][1]
                        tt("dve", big[:, 3, lo:hi], big[:, 1, lo:hi], big[:, 2, lo:hi], ALU.mult,
                           allb(1) + allb(2), allb(3))
                        conv3("dve", big[:, 1, :], big[:, 3, :], smalls[:, o_c:o_c + 3], seglist, allb(3) + (sm_b,), allb(1))
                        tt("dve", big[:, 4 + j, lo:hi], big[:, 0, lo:hi], big[:, 1, lo:hi], ALU.mult,
                           allb(0) + allb(1), allb(4 + j))

                ck(l, 4)
                rg, rgb = load_w(win_d[l, 5].rearrange("p k c -> p (k c)"))
                rg3 = rg.rearrange("p (k c) -> p k c", k=8)
                xp = big[:, 0:2, :].rearrange("p a b -> p (a b)").rearrange("p (t c) -> p t c", t=18)
                for seg in full_segs:
                    s0, n = SEGS[seg]
                    for i0 in range(s0 // 128, (s0 + n) // 128, 2):
                        bk, bkb = bka()
                        c0t = 0 if seg == "ctx" else CTX + ((i0 - 2) // 4) * 512
                        for j in range(2):
                            i = i0 + j
                            mm(bk[:, j * 256:(j + 1) * 256],
                               [(hT[:, k, i * 128:(i + 1) * 128], rg3[:, k, 0:256]) for k in range(8)],
                               (rgb, hb(c0t)), (bkb,))
                        act(xp[:, i0:i0 + 2, :].rearrange("p t c -> p (t c)"), bk[:, 0:512], AF.Identity, (bkb,),
                            (BG(i0 * 256, i0 * 256 + 512),))
                for seg in full_segs:
                    s0, n = SEGS[seg]
                    t_lo, t_hi = s0 // 128, (s0 + n) // 128
                    for ch in range(2):
                        for i0 in range(t_lo, t_hi, 2):
                            bk, bkb = bka()
                            rd = set()
                            for jj in range(2):
                                i = i0 + jj
                                for g in range(2):
                                    wi = 2 * ch + g
                                    pairs = []
                                    for dlt, v in ((-1, 0), (0, 1), (1, 2)):
                                        ii = i + dlt
                                        if ii < t_lo or ii >= t_hi:
                                            continue
                                        vv = v
                                        if dlt == 0 and i == t_lo:
                                            vv = 3
                                        if dlt == 0 and i == t_hi - 1:
                                            vv = 4
                                        pairs.append((xp[:, ii, ch * 128:(ch + 1) * 128],
                                                      cbs(CB_BAND + (wi * 5 + vv) * 128)))
                                        rd.update(BG(ii * 256, ii * 256 + 256))
                                    mm(bk[:, (jj * 2 + g) * 128:(jj * 2 + g + 1) * 128], pairs, tuple(rd) + (cb_b,), (bkb,))
                            c0t = 0 if seg == "ctx" else CTX + ((i0 - 2) // 4) * 512
                            bk4 = bk[:, :].rearrange("p (j g n) -> p j g n", j=2, g=2)
                            pdst = big[:, 2 + ch, i0 * 128:(i0 + 2) * 128].rearrange("p (j n) -> p j n", j=2)
                            pb_ = BG((2 + ch) * NT + i0 * 128, (2 + ch) * NT + i0 * 128 + 256)
                            P.op("act", lambda e, pdst=pdst, bk4=bk4: e.activation(out=pdst[0:64], in_=bk4[0:64, :, 0, :], func=AF.Identity),
                                 (bkb,), (pb_,))
                            P.op("dve", lambda e, pdst=pdst, bk4=bk4: e.tensor_copy(pdst[64:128], bk4[64:128, :, 1, :]),
                                 (bkb,), (pb_,))
                    for ch in range(2):
                        o_s = _off["pscale"][0] + l * 2 + ch
                        for (sg, c0, W) in tiles_of(seg):
                            bk, bkb = bka()
                            rd = BG((2 + ch) * NT + c0, (2 + ch) * NT + c0 + W)
                            mm(bk[:, 0:W], [(bdpm[:, ch, :], big[:, 2 + ch, c0:c0 + W])], rd + (buf("bdpm"),), (bkb,))
                            act(pbuf[:, ch, c0:c0 + W], bk[:, 0:W], AF.Identity, (bkb, sm_b), (pbb(ch, c0),),
                                scale=smalls[:, o_s:o_s + 1])

                ck(l, 5)
                for blk in (6, 7):
                    rg, rgb = load_w(win_d[l, blk].rearrange("p k c -> p (k c)"))
                    rg3 = rg.rearrange("p (k c) -> p k c", k=8)
                    for half in range(2):
                        j = (blk - 6) * 2 + half
                        for (seg, c0, W) in full_tiles:
                            bk, bkb = proj_fm(rg3, rgb, half, c0, W)
                            headnorm_rope(bk, bkb, seg, c0, W, l, 0, big[:, j, c0:c0 + W], bigb(j, c0))

                ck(l, 6)
                SP = [(0, 1), (2, 3)]
                AA, AB, DA, DB = 4, 5, 6, 7
                one1 = cbs(CB_ONE)
                pairs = []
                chunks = []
                for seg in full_segs:
                    kts = [0, 1] if seg == "ctx" else list(range(2, 18)) + [0, 1]
                    for (sg, c0, W) in tiles_of(seg):
                        for j in range(4):
                            ci = len(chunks)
                            chunks.append((seg, c0, W, j))
                            for ki, kt in enumerate(kts):
                                pairs.append((ci, kt, ki == 0, ki == len(kts) - 1))

                def emit_S(pi):
                    ci, kt, first, last_ = pairs[pi]
                    seg, c0, W, j = chunks[ci]
                    kc0 = 0 if kt < 2 else CTX + ((kt - 2) // 4) * 512
                    for hh in range(2):
                        rows = slice(hh * 64, (hh + 1) * 64)
                        b_ = SP[pi % 2][hh]
                        mm(banks[b_][:, 0:W], [(KT[rows, kt * 128:(kt + 1) * 128], big[rows, j, c0:c0 + W])],
                           (KTB(kc0), bigb(j, c0)), (bank_b[b_],))

                emit_S(0)
                for pi in range(len(pairs)):
                    if pi + 1 < len(pairs):
                        emit_S(pi + 1)
                    ci, kt, first, last_ = pairs[pi]
                    seg, c0, W, j = chunks[ci]
                    pts = []
                    for hh in range(2):
                        b_ = SP[pi % 2][hh]
                        pt, ptb = pta()
                        act(pt[:, 0:W], banks[b_][:, 0:W], AF.Exp, (bank_b[b_],), (ptb,), scale=0.125)
                        pts.append((pt, ptb))
                    for hh in range(2):
                        pt, ptb = pts[hh]
                        ab = (AA, AB)[hh]
                        db = (DA, DB)[hh]
                        mm(banks[ab][:, 0:W], [(Vt[:, kt, :], pt[:, 0:W])], (ptb, V_b), (bank_b[ab],), first=first, last=last_)
                        mm(banks[db][:, 0:W], [(one1, pt[:, 0:W])], (ptb, cb_b), (bank_b[db],), first=first, last=last_)
                    if (pending_mod or mod_loaded) and pi % 7 == 6:
                        mod_step(SP[pi % 2][0])
                    if last_:
                        Tn, Tnb = rsa()
                        Td, Tdb = tfa()
                        P.op("dve", lambda e, Tn=Tn, W=W: e.tensor_copy(Tn[0:64, 0:W], banks[AA][0:64, 0:W]), (bank_b[AA],), (Tnb,))
                        P.op("dve", lambda e, Td=Td, W=W: e.tensor_copy(Td[0:64, 0:W], banks[DA][0:64, 0:W]), (bank_b[DA],), (Tdb,))
                        act(Tn[64:128, 0:W], banks[AB][64:128, 0:W], AF.Identity, (bank_b[AB],), (Tnb,))
                        act(Td[64:128, 0:W], banks[DB][64:128, 0:W], AF.Identity, (bank_b[DB],), (Tdb,))
                        P.op("dve", lambda e, Td=Td, W=W: e.reciprocal(Td[:, 0:W], Td[:, 0:W]), (Tdb,), (Tdb,))
                        tt("pool", Tn[:, 0:W], Tn[:, 0:W], Td[:, 0:W], ALU.mult, (Tnb, Tdb), (Tnb,))
                        P.op("pool", lambda e, Tn=Tn, j=j, c0=c0, W=W: e.tensor_copy(big[:, j, c0:c0 + W], Tn[:, 0:W]),
                             (Tnb,), (bigb(j, c0),))

                while pending_mod or mod_loaded:
                    mod_step()
                layer_vecs(l, 1)
                ck(l, 7)
                brin = [(big, 0), (big, 1), (big, 2), (big, 3), (big, 4), (big, 5), (big, 6), (big, 7), (pbuf, 0), (pbuf, 1)]
                broff = [(0, 4), (4, 2), (6, 2), (8, 2)]

                def brb(kk, c0):
                    t_, ch = brin[kk]
                    return bigb(ch, c0) if t_ is big else pbb(ch, c0)

                p4slots = [(ring[:, i, :], ring_b[i]) for i in range(NRING)]
                p4slots.append((KTt[:, 0:2048], KT_ALL))
                p4slots.append((Vt[:, :, :].rearrange("p t c -> p (t c)")[:, 0:2048], V_b))
                p4slots.append((mreg[:, 0:2, :].rearrange("p a b -> p (a b)"), (mh_b[0], mh_b[1], mh_b[2], mh_b[3])))
                p4cnt = [0]

                def p4load(src_ap, n):
                    ap_, b_ = p4slots[p4cnt[0] % len(p4slots)]
                    p4cnt[0] += 1
                    dma("pool", ap_[:, 0:n], src_ap, (), (b_,))
                    return ap_, b_

                prev = None
                for c in range(8):
                    g0, g0b = p4load(win_d[l, 8 + 2 * c].rearrange("p k c -> p (k c)"), 2048)
                    g1, g1b = p4load(win_d[l, 9 + 2 * c].rearrange("p k c -> p (k c)"), 2048)
                    gsl = [(g0.rearrange("p (k c) -> p k c", k=8), g0b, 0), (g0.rearrange("p (k c) -> p k c", k=8), g0b, 1),
                           (g1.rearrange("p (k c) -> p k c", k=8), g1b, 0), (g1.rearrange("p (k c) -> p k c", k=8), g1b, 1)]
                    wb, wbb = p4load(wbr_d[l, c].rearrange("p k c -> p (k c)"), 1280)
                    wb3 = wb[:, 0:1280].rearrange("p (k c) -> p k c", k=10)
                    wo, wob = p4load(wout_d[l, c], 1024)

                    def stage_b(pv):
                        seg, c0, W, yt, ytb, wo, wob = pv
                        s_ = 0 if seg == "lat" else 1
                        for dch in range(8):
                            bw, bwb = bka()
                            mm(bw[:, 0:W], [(wo[:, dch * 128:(dch + 1) * 128], yt[:, 0:W])], (wob, ytb), (bwb,))
                            r_ap = resid(seg, dch, c0, W)
                            stt("dve", r_ap, bw[:, 0:W], vecs[:, s_, 2, dch:dch + 1], r_ap, ALU.mult, ALU.add,
                                (bwb, vec_b, rb(seg, dch, c0)), (rb(seg, dch, c0),))

                    for ti_, (seg, c0, W) in enumerate(full_tiles):
                        acc, accb = rsa()
                        yi = (c * len(full_tiles) + ti_) % 2
                        yt, ytb = ytp[:, yi, :], ytp_b[yi]
                        for i in range(4):
                            rg3, rgb, half = gsl[i]
                            bk, bkb = proj_fm(rg3, rgb, half, c0, W)
                            gt, gtb = tba()
                            act(gt[:, 0:W], bk[:, 0:W], AF.Sigmoid, (bkb,), (gtb,))
                            k0, nk = broff[i]
                            bo, bob = bka()
                            pairs_ = []
                            rd = []
                            for kk in range(k0, k0 + nk):
                                t_, ch = brin[kk]
                                pairs_.append((wb3[:, kk, :], t_[:, ch, c0:c0 + W]))
                                rd.append(brb(kk, c0))
                            mm(bo[:, 0:W], pairs_, tuple(rd) + (wbb,), (bob,))
                            if i == 0:
                                tt("dve", acc[:, 0:W], bo[:, 0:W], gt[:, 0:W], ALU.mult, (bob, gtb), (accb,))
                            else:
                                t2, t2b = tfa()
                                tt("dve", t2[:, 0:W], bo[:, 0:W], gt[:, 0:W], ALU.mult, (bob, gtb), (t2b,))
                                if i < 3:
                                    tt("pool", acc[:, 0:W], acc[:, 0:W], t2[:, 0:W], ALU.add, (accb, t2b), (accb,))
                                else:
                                    tt("pool", yt[:, 0:W], acc[:, 0:W], t2[:, 0:W], ALU.add, (accb, t2b), (ytb,))
                        if prev is not None:
                            stage_b(prev)
                        prev = (seg, c0, W, yt, ytb, wo, wob)
                stage_b(prev)

            ck(l, 8)
            for seg in full_segs:
                norm_phase(seg, 1)
            ck(l, 8.5)
            ca = KTt[:, 0:NT]
            cbv = Vflat[:, 0:NT]
            lo = seglist[0][0]
            hi = seglist[-1][0] + seglist[-1][1]
            ua_b = tuple(pbb(0, c0) for (_, c0, _) in full_tiles)
            ub_b = tuple(pbb(1, c0) for (_, c0, _) in full_tiles)
            ca_b = KT_ALL
            cb2_b = V_b
            def down_proj(w3, njj):
                for (seg, c0, W) in full_tiles:
                    s = 0 if seg == "lat" else 1
                    for dch in range(8):
                        bw, bwb = bka()
                        mm(bw[:, 0:W], [(w3[:, jj, dch * 128:(dch + 1) * 128], big[:, jj, c0:c0 + W]) for jj in range(njj)],
                           tuple(bigb(jj, c0) for jj in range(njj)) + (wsb_of[id(w3)],), (bwb,))
                        r_ap = resid(seg, dch, c0, W)
                        stt("dve", r_ap, bw[:, 0:W], vecs[:, s, 5, dch:dch + 1], r_ap, ALU.mult, ALU.add,
                            (bwb, vec_b, rb(seg, dch, c0)), (rb(seg, dch, c0),))

            wsb_of = {}
            pendingD = None
            for g in range(NGRP):
                njj = min(GSZ, NFF - g * GSZ)
                wslot = big[:, 4 + 2 * (g % 2):6 + 2 * (g % 2), :].rearrange("p a b -> p (a b)")
                wsb = BG((4 + 2 * (g % 2)) * NT, (4 + 2 * (g % 2)) * NT + GSZ * 1024)
                dma("pool", wslot[:, 0:GSZ * 1024], wdn_d[l, g].rearrange("p j c -> p (j c)"), (), (wsb,))
                w3 = wslot[:, 0:GSZ * 1024].rearrange("p (j c) -> p j c", j=GSZ)
                wsb_of[id(w3)] = wsb
                for jj in range(njj):
                    j = g * GSZ + jj
                    rg, rgb = load_w(wup_d[l, j].rearrange("p k c -> p (k c)"))
                    rg3 = rg.rearrange("p (k c) -> p k c", k=8)
                    for (seg, c0, W) in full_tiles:
                        bk, bkb = proj_fm(rg3, rgb, 0, c0, W)
                        act(pbuf[:, 0, c0:c0 + W], bk[:, 0:W], AF.Identity, (bkb,), (pbb(0, c0),))
                        bk, bkb = proj_fm(rg3, rgb, 1, c0, W)
                        act(pbuf[:, 1, c0:c0 + W], bk[:, 0:W], AF.Identity, (bkb,), (pbb(1, c0),))
                    if jj == 0 and pendingD is not None:
                        down_proj(*pendingD)
                        pendingD = None
                    o_a = _off["wcf"][0] + (l * 44 + j) * 3
                    o_b = _off["wcf"][0] + (l * 44 + 22 + j) * 3
                    conv3("dve", ca, pbuf[:, 0, :], smalls[:, o_a:o_a + 3], seglist, (ua_b, sm_b), (ca_b,))
                    conv3("dve", cbv, pbuf[:, 1, :], smalls[:, o_b:o_b + 3], seglist, (ub_b, sm_b), (cb2_b,))
                    act(ca[:, lo:hi], ca[:, lo:hi], AF.Silu, (ca_b,), (ca_b,))
                    tt("dve", big[:, jj, lo:hi], ca[:, lo:hi], cbv[:, lo:hi], ALU.mult, (ca_b, cb2_b), (BG(jj * NT + lo, jj * NT + hi),))
                pendingD = (w3, njj)
            down_proj(*pendingD)
            ck(l, 9)

        try:
            if limit == 0:
                raise _Stop()
            for l in range(DEPTH):
                layer(l)
        except _Stop:
            pass

        out_ops = []
        if dbg_d is not None:
            o_ = 0
            for di, (fn_, n_) in enumerate(dbg[0]):
                out_ops.append(dma("sp", dbg_d[:, o_:o_ + n_], fn_(locals()), tuple(B.values()), (buf(f"dbgout{di}"),) + tuple(B.values())))
                o_ += n_
        ostage = big[:, :, :].rearrange("p a b -> p (a b)").bitcast(F32)
        o_f = _off["fnorm"][0]
        for ti, (seg, c0, W) in enumerate(tiles_of("lat")):
            bk, bkb = bka()
            for k in range(8):
                sq, sqb = tba()
                act(sq[:, 0:W], resid(seg, k, c0, W), AF.Square, (rb(seg, k, c0),), (sqb,))
                mm(bk[:, 0:W], [(ones_ap, sq[:, 0:W])], (sqb, cb_b), (bkb,), first=(k == 0), last=(k == 7))
            r, rbf = rstd_from(bk[:, 0:W], bkb, W)
            osl = ti % 2
            ost = ostage[:, osl * 4096:(osl + 1) * 4096].rearrange("p (s d) -> p s d", s=4)
            osb = BG(osl * 8192, osl * 8192 + 8192)
            for k in range(8):
                t, tbf = tfa()
                stt("dve", t[:, 0:W], resid(seg, k, c0, W), smalls[:, o_f + k:o_f + k + 1], r[:, 0:W], ALU.mult, ALU.mult,
                    (rb(seg, k, c0), rbf, sm_b), (tbf,))
                b2, b2b = bka()
                def fnO(e, t=t, b2=b2):
                    ins = None
                    for sidx in range(4):
                        ins = e.transpose(b2[:, sidx * 128:(sidx + 1) * 128], t[:, sidx * 128:(sidx + 1) * 128], identf[:, :])
                    return ins
                P.op("pe", fnO, (tbf, idf_b), (b2b,))
                act(ost[:, :, k * 128:(k + 1) * 128], b2[:, 0:512].rearrange("p (s d) -> p s d", s=4), AF.Identity,
                    (b2b,), (osb,))
            t0 = (c0 - CTX)
            out_ops.append(dma("sp", out_d[t0:t0 + 512, :].rearrange("(s p) d -> p s d", p=128), ost, (osb,), (buf(f"outd{ti}"),)))

        eng_sem = {e: es.enter_context(nc.semaphore(f"sem_{e}")) for e in Prog.ENGS}
        NDS = 8
        dma_sems = {q: [es.enter_context(nc.semaphore(f"dsem_{q}{i}")) for i in range(NDS)] for q in ("sp", "pool")}
        for e in Prog.ENGS:
            c_ = 0
            nd = 0
            for o in P.ops[e]:
                if o.is_dma:
                    o.sem = dma_sems[e][nd % NDS]
                    o.val = 16 * (nd // NDS + 1)
                    o.prev_val = 16 * (nd // NDS)
                    nd += 1
                elif o.needed:
                    c_ += 1
                    o.sem = eng_sem[e]
                    o.val = c_
        handles = {"pe": None, "act": None, "dve": None, "pool": None, "sp": None}

        def emit(ename, h):
            waited = {}
            for o in P.ops[ename]:
                need = {}
                for d in o.deps:
                    k = id(d.sem)
                    if k not in need or need[k][1] < d.val:
                        need[k] = (d.sem, d.val)
                if o.is_dma and o.prev_val > 0:
                    k = id(o.sem)
                    if k not in need or need[k][1] < o.prev_val:
                        need[k] = (o.sem, o.prev_val)
                for k, (s_, v_) in need.items():
                    if waited.get(k, 0) < v_:
                        h.wait_ge(s_, v_)
                        waited[k] = v_
                ins = o.fn(h)
                if o.is_dma:
                    ins.then_inc(o.sem, 16)
                elif o.needed:
                    ins.then_inc(o.sem, 1)
            if ename == "sp":
                for o in out_ops:
                    if waited.get(id(o.sem), 0) < o.val:
                        h.wait_ge(o.sem, o.val)
                        waited[id(o.sem)] = o.val

        with nc.Block() as block:
            @block.sync
            def _(e):
                emit("sp", e)

            @block.tensor
            def _(e):
                emit("pe", e)

            @block.scalar
            def _(e):
                emit("act", e)

            @block.vector
            def _(e):
                emit("dve", e)

            @block.gpsimd
            def _(e):
                emit("pool", e)
    return nc


_CACHE = {}


def kernel(**inputs):
    sh = _prep_shared(inputs)
    xs = np.asarray(inputs["x"], dtype=np.float32)
    cs = np.asarray(inputs["ctx"], dtype=np.float32)
    in_maps = []
    for b in range(8):
        m = dict(sh)
        m["x"] = np.ascontiguousarray(xs[b])
        m["ctx"] = np.ascontiguousarray(cs[b])
        m["smalls"], m["bdpm"] = _smalls(inputs, b)
        in_maps.append(m)
    nc = build_nc()
    res = run_bass_kernel_spmd(nc, in_maps, core_ids=list(range(8)))
    out = np.stack([np.asarray(res.results[b]["out"], dtype=np.float32) for b in range(8)], axis=0)
    return out
```
